# Optimizing a Trainium2 kernel written in Bass

```python
import jax, jax.numpy as jnp
from jax import lax
import numpy as np

D_MODEL = 2048
BATCH = 4
SEQ = 4096
DEPTH = 2

GRID_W = 64
CTX_LEN = 256
N_EVEN = (DEPTH + 1) // 2
N_ODD = DEPTH // 2
N_MOD = 6
MIX_W = D_MODEL
GLA_HEADS = 4
GLA_VAL_W = MIX_W // 2
GLA_KEY_W = GLA_VAL_W // 2
GLA_DK = GLA_KEY_W // GLA_HEADS
GLA_DV = GLA_VAL_W // GLA_HEADS
GLA_GATE_RANK = 16
GLA_GATE_TAU = 16.0
GLA_CHUNK = 64
LRU_W = MIX_W - GLA_VAL_W
LRU_BLOCKS = 8
LRU_BW = LRU_W // LRU_BLOCKS
LRU_CONV_W = 4
LRU_C = 8.0
IN_EVEN = 2 * GLA_KEY_W + 2 * GLA_VAL_W + 2 * GLA_GATE_RANK + 2 * LRU_W
NA_HEADS = 16
NA_HD = D_MODEL // NA_HEADS
NA_WIN_ROWS = 8
NA_WIN_COLS = 16
D_FF = 4 * D_MODEL
ROPE_BASE = 10000.0
EPS = 1e-6

kernel_name = "hybrid_gla_rglru_natten_dit"

F32 = jnp.float32


def rms_norm(x, g):
    xf = x.astype(F32)
    y = xf * lax.rsqrt(jnp.mean(xf * xf, axis=-1, keepdims=True) + EPS)
    return (y * g.astype(F32)).astype(x.dtype)


def modulate(x, g, shift, scale):
    return rms_norm(x, g) * (1 + scale) + shift


def axial_rope(x):
    n_tok, d = x.shape[1], x.shape[-1]
    d_axis = d // 2
    n_freq = d_axis // 2
    inv_freq = ROPE_BASE ** (-jnp.arange(n_freq, dtype=F32) / n_freq)
    t = jnp.arange(n_tok)
    row = (t // GRID_W).astype(F32)
    col = (t % GRID_W).astype(F32)

    def rot(xh, pos):
        ang = pos[:, None] * inv_freq[None, :]
        cos = jnp.cos(ang)[None, :, None, :]
        sin = jnp.sin(ang)[None, :, None, :]
        x1, x2 = xh[..., :n_freq], xh[..., n_freq:]
        return jnp.concatenate([x1 * cos - x2 * sin, x1 * sin + x2 * cos], axis=-1)

    xf = x.astype(F32)
    out = jnp.concatenate([rot(xf[..., :d_axis], row), rot(xf[..., d_axis:], col)], axis=-1)
    return out.astype(x.dtype)


def gla_scan(q, k, v, g, s0):
    bsz, L, H, _ = q.shape
    dv = v.shape[-1]
    n = L // GLA_CHUNK

    def chunks(t):
        return t.astype(F32).reshape(bsz, n, GLA_CHUNK, H, t.shape[-1]).transpose(1, 0, 3, 2, 4)

    lower = jnp.tril(jnp.ones((GLA_CHUNK, GLA_CHUNK), dtype=bool))[:, :, None]

    def body(s, inp):
        qc, kc, vc, gc = inp
        b = jnp.cumsum(gc, axis=-2)
        o = jnp.einsum('bhtd,bhdv->bhtv', qc * jnp.exp(b), s)
        w = jnp.exp(jnp.where(lower, b[..., :, None, :] - b[..., None, :, :], -jnp.inf))
        att = jnp.einsum('bhtd,bhsd,bhtsd->bhts', qc, kc, w)
        o = o + jnp.einsum('bhts,bhsv->bhtv', att, vc)
        b_last = b[..., -1:, :]
        s = jnp.exp(b_last)[..., 0, :, None] * s + jnp.einsum('bhsd,bhsv->bhdv', kc * jnp.exp(b_last - b), vc)
        return s, o

    s_fin, o = lax.scan(body, s0.astype(F32), (chunks(q), chunks(k), chunks(v), chunks(g)))
    o = o.transpose(1, 0, 3, 2, 4).reshape(bsz, L, H, dv)
    return o, s_fin


def gla_bidir(q, k, v, g_f, g_b, s0_f, s0_b):
    o_f, s_f = gla_scan(q, k, v, g_f, s0_f)
    fl = lambda t: jnp.flip(t, axis=1)
    o_b, s_b = gla_scan(fl(q), fl(k), fl(v), fl(g_b), s0_b)
    return o_f + fl(o_b), s_f, s_b


def linear_scan(a, b, h0):
    def combine(e1, e2):
        a1, b1 = e1
        a2, b2 = e2
        return a1 * a2, a2 * b1 + b2
    a_acc, b_acc = lax.associative_scan(combine, (a, b), axis=1)
    return b_acc + a_acc * h0[:, None, :]


def rglru_dir(xc, w_a, b_a, w_x, b_x, lam, h0):
    bsz, L, W = xc.shape
    xb = xc.reshape(bsz, L, LRU_BLOCKS, LRU_BW)
    r = jax.nn.sigmoid(jnp.einsum('blgi,gij->blgj', xb, w_a.astype(F32)).reshape(bsz, L, W) + b_a.astype(F32))
    i = jax.nn.sigmoid(jnp.einsum('blgi,gij->blgj', xb, w_x.astype(F32)).reshape(bsz, L, W) + b_x.astype(F32))
    log_a = LRU_C * r * jax.nn.log_sigmoid(lam.astype(F32))
    a = jnp.exp(log_a)
    u = jnp.sqrt(-jnp.expm1(2 * log_a)) * (i * xc)
    return linear_scan(a, u, h0)


def dwconv_centred(x, w, b):
    L = x.shape[1]
    left = LRU_CONV_W // 2
    right = LRU_CONV_W - 1 - left
    xp = jnp.pad(x, ((0, 0), (left, right), (0, 0)))
    out = b + xp[:, 0:L] * w[0]
    for j in range(1, LRU_CONV_W):
        out = out + xp[:, j:j + L] * w[j]
    return out


def even_mixer(h_lat, h_ctx, w_in, w_out, gate_up, gate_b, gla_g, conv_w, conv_b,
               w_a, b_a, w_x, b_x, lam, need_ctx):
    sizes = [GLA_KEY_W, GLA_KEY_W, GLA_VAL_W, GLA_VAL_W, 2 * GLA_GATE_RANK, LRU_W, LRU_W]
    splits = np.cumsum(sizes)[:-1].tolist()

    def prep(h, use_rope):
        bsz, L, _ = h.shape
        q, k, v, o_gate, g_low, x_r, y_r = jnp.split(h @ w_in, splits, axis=-1)
        q = q.reshape(bsz, L, GLA_HEADS, GLA_DK) * GLA_DK ** -0.5
        k = k.reshape(bsz, L, GLA_HEADS, GLA_DK)
        if use_rope:
            q, k = axial_rope(q), axial_rope(k)
        v = v.reshape(bsz, L, GLA_HEADS, GLA_DV)
        g_low = g_low.reshape(bsz, L, 2, GLA_GATE_RANK)
        z = (jnp.einsum('blnr,nrk->blnk', g_low, gate_up) + gate_b).astype(F32)
        g = (jax.nn.log_sigmoid(z) / GLA_GATE_TAU).reshape(bsz, L, 2, GLA_HEADS, GLA_DK)
        xc = dwconv_centred(x_r, conv_w, conv_b).astype(F32)
        return q, k, v, g[:, :, 0], g[:, :, 1], o_gate, xc, y_r

    def lru_both(xc, h0f, h0b):
        hf = rglru_dir(xc, w_a[0], b_a[0], w_x[0], b_x[0], lam[0], h0f)
        hb = jnp.flip(rglru_dir(jnp.flip(xc, axis=1), w_a[1], b_a[1], w_x[1], b_x[1], lam[1], h0b), axis=1)
        return hf, hb

    def merge(h, o, o_gate, hf, hb, y_r):
        bsz, L = o.shape[:2]
        a_out = rms_norm(o, gla_g.reshape(GLA_HEADS, GLA_DV)).reshape(bsz, L, GLA_VAL_W) * jax.nn.silu(o_gate)
        b_out = (hf + hb) * jax.nn.gelu(y_r)
        return (jnp.concatenate([a_out, b_out], axis=-1) @ w_out).astype(h.dtype)

    bsz = h_lat.shape[0]
    qc, kc, vc, gfc, gbc, ogc, xcc, ycc = prep(h_ctx, False)
    ql, kl, vl, gfl, gbl, ogl, xcl, ycl = prep(h_lat, True)
    zero_s = jnp.zeros((bsz, GLA_HEADS, GLA_DK, GLA_DV), F32)
    o_ctx, s_f, s_b = gla_bidir(qc, kc, vc, gfc, gbc, zero_s, zero_s)
    o_lat, _, _ = gla_bidir(ql, kl, vl, gfl, gbl, s_f, s_b)
    zero_h = jnp.zeros((bsz, LRU_W), F32)
    hf_c, hb_c = lru_both(xcc, zero_h, zero_h)
    hf_l, hb_l = lru_both(xcl, hf_c[:, -1], hb_c[:, 0])
    y_lat = merge(h_lat, o_lat, ogl, hf_l, hb_l, ycl)
    y_ctx = merge(h_ctx, o_ctx, ogc, hf_c, hb_c, ycc) if need_ctx else None
    return y_lat, y_ctx


def na_mixer(h_lat, h_ctx, w_qkv, w_out, rel_bias, need_ctx):
    bsz, L, _ = h_lat.shape
    rows = L // GRID_W
    win_r = min(NA_WIN_ROWS, rows)
    n_ctx = h_ctx.shape[1]

    def heads(t, n):
        return t.reshape(bsz, n, NA_HEADS, NA_HD).transpose(0, 2, 1, 3)

    q_l, k_l, v_l = jnp.split(h_lat @ w_qkv, 3, axis=-1)
    q_l = heads(q_l, L) * NA_HD ** -0.5
    k_l, v_l = heads(k_l, L), heads(v_l, L)
    k_c, v_c = jnp.split(h_ctx @ w_qkv[:, D_MODEL:], 2, axis=-1)
    k_c, v_c = heads(k_c, n_ctx), heads(v_c, n_ctx)

    grid = lambda t: t.reshape(bsz, NA_HEADS, rows, GRID_W, NA_HD)
    qg, kg, vg = grid(q_l), grid(k_l), grid(v_l)

    col = jnp.arange(GRID_W)
    col_start = jnp.clip(col - NA_WIN_COLS // 2, 0, GRID_W - NA_WIN_COLS)
    col_mask = (col[None, :] >= col_start[:, None]) & (col[None, :] < col_start[:, None] + NA_WIN_COLS)
    dc_idx = jnp.clip(col[None, :] - col[:, None] + NA_WIN_COLS - 1, 0, 2 * NA_WIN_COLS - 2)
    bias_cols = rel_bias.astype(F32)[:, :, dc_idx]
    mask = jnp.tile(col_mask, (1, win_r))
    n_band = win_r * GRID_W

    def row_block(r):
        rs = jnp.clip(r - win_r // 2, 0, rows - win_r)
        k_band = lax.dynamic_slice_in_dim(kg, rs, win_r, axis=2).reshape(bsz, NA_HEADS, n_band, NA_HD)
        v_band = lax.dynamic_slice_in_dim(vg, rs, win_r, axis=2).reshape(bsz, NA_HEADS, n_band, NA_HD)
        q_row = lax.dynamic_index_in_dim(qg, r, axis=2, keepdims=False)
        dr_idx = rs + jnp.arange(win_r) - r + NA_WIN_ROWS - 1
        bias = jnp.take(bias_cols, dr_idx, axis=1).transpose(0, 2, 1, 3).reshape(NA_HEADS, GRID_W, n_band)
        s_lat = jnp.einsum('bhqd,bhkd->bhqk', q_row, k_band).astype(F32) + bias
        s_lat = jnp.where(mask, s_lat, -jnp.inf)
        s_ctx = jnp.einsum('bhqd,bhkd->bhqk', q_row, k_c).astype(F32)
        p = jax.nn.softmax(jnp.concatenate([s_lat, s_ctx], axis=-1), axis=-1).astype(v_band.dtype)
        return (jnp.einsum('bhqk,bhkd->bhqd', p[..., :n_band], v_band)
                + jnp.einsum('bhqk,bhkd->bhqd', p[..., n_band:], v_c))

    o = lax.map(row_block, jnp.arange(rows))
    o = o.transpose(1, 0, 3, 2, 4).reshape(bsz, L, D_MODEL)
    y_lat = o @ w_out
    y_ctx = None
    if need_ctx:
        q_c = heads(h_ctx @ w_qkv[:, :D_MODEL], n_ctx) * NA_HD ** -0.5
        p_c = jax.nn.softmax(jnp.einsum('bhqd,bhkd->bhqk', q_c, k_c).astype(F32), axis=-1).astype(v_c.dtype)
        o_c = jnp.einsum('bhqk,bhkd->bhqd', p_c, v_c).transpose(0, 2, 1, 3).reshape(bsz, n_ctx, D_MODEL)
        y_ctx = o_c @ w_out
    return y_lat, y_ctx


def sq_relu_mlp(h, w_up, w_down):
    return jnp.square(jax.nn.relu(h @ w_up)) @ w_down


def setup_inputs(seed: int = 0) -> dict:
    key = jax.random.key(seed)
    ks = iter(jax.random.split(key, 40))
    D = D_MODEL

    def nrm(shape, scale):
        return jax.random.normal(next(ks), shape, F32) * scale

    lam_u = jax.random.uniform(next(ks), (N_EVEN, 2, LRU_W), F32, minval=0.9, maxval=0.999)
    root = lam_u ** (1.0 / LRU_C)
    lru_lambda = jnp.log(root) - jnp.log1p(-root)
    return {
        "x": nrm((BATCH, SEQ, D), 1.0),
        "c": nrm((BATCH, D), 1.0),
        "ctx": nrm((BATCH, CTX_LEN, D), 1.0),
        "c_ctx": nrm((D,), 1.0),
        "mod_w": nrm((DEPTH, D, N_MOD * D), 0.5 * D ** -0.5),
        "mod_b": nrm((DEPTH, N_MOD * D), 0.01),
        "norm1_g": 1.0 + nrm((DEPTH, D), 0.02),
        "norm2_g": 1.0 + nrm((DEPTH, D), 0.02),
        "mlp_w_up": nrm((DEPTH, D, D_FF), D ** -0.5),
        "mlp_w_down": nrm((DEPTH, D_FF, D), D_FF ** -0.5),
        "ev_w_in": nrm((N_EVEN, D, IN_EVEN), D ** -0.5),
        "ev_w_out": nrm((N_EVEN, MIX_W, D), MIX_W ** -0.5),
        "gla_gate_up": nrm((N_EVEN, 2, GLA_GATE_RANK, GLA_KEY_W), GLA_GATE_RANK ** -0.5),
        "gla_gate_b": nrm((N_EVEN, 2, GLA_KEY_W), 0.1),
        "gla_norm_g": 1.0 + nrm((N_EVEN, GLA_VAL_W), 0.02),
        "lru_conv_w": nrm((N_EVEN, LRU_CONV_W, LRU_W), LRU_CONV_W ** -0.5),
        "lru_conv_b": nrm((N_EVEN, LRU_W), 0.01),
        "lru_w_a": nrm((N_EVEN, 2, LRU_BLOCKS, LRU_BW, LRU_BW), LRU_BW ** -0.5),
        "lru_b_a": nrm((N_EVEN, 2, LRU_W), 0.01),
        "lru_w_x": nrm((N_EVEN, 2, LRU_BLOCKS, LRU_BW, LRU_BW), LRU_BW ** -0.5),
        "lru_b_x": nrm((N_EVEN, 2, LRU_W), 0.01),
        "lru_lambda": lru_lambda,
        "na_w_qkv": nrm((N_ODD, D, 3 * D), D ** -0.5),
        "na_w_out": nrm((N_ODD, D, D), D ** -0.5),
        "na_rel_bias": nrm((N_ODD, NA_HEADS, 2 * NA_WIN_ROWS - 1, 2 * NA_WIN_COLS - 1), 0.02),
        "final_norm_g": 1.0 + nrm((D,), 0.02),
    }


def reference(x, c, ctx, c_ctx, mod_w, mod_b, norm1_g, norm2_g, mlp_w_up, mlp_w_down,
              ev_w_in, ev_w_out, gla_gate_up, gla_gate_b, gla_norm_g, lru_conv_w, lru_conv_b,
              lru_w_a, lru_b_a, lru_w_x, lru_b_x, lru_lambda, na_w_qkv, na_w_out, na_rel_bias,
              final_norm_g):
    silu_c = jax.nn.silu(c)
    silu_cc = jax.nn.silu(c_ctx)
    for layer in range(DEPTH):
        need_ctx = layer < DEPTH - 1
        mod_l = (silu_c @ mod_w[layer] + mod_b[layer])[:, None, :]
        mod_c = silu_cc @ mod_w[layer] + mod_b[layer]
        sh1, sc1, g1, sh2, sc2, g2 = jnp.split(mod_l, N_MOD, axis=-1)
        csh1, csc1, cg1, csh2, csc2, cg2 = jnp.split(mod_c, N_MOD, axis=-1)
        h_lat = modulate(x, norm1_g[layer], sh1, sc1)
        h_ctx = modulate(ctx, norm1_g[layer], csh1, csc1)
        if layer % 2 == 0:
            i = layer // 2
            y_lat, y_ctx = even_mixer(h_lat, h_ctx, ev_w_in[i], ev_w_out[i], gla_gate_up[i], gla_gate_b[i],
                                      gla_norm_g[i], lru_conv_w[i], lru_conv_b[i], lru_w_a[i], lru_b_a[i],
                                      lru_w_x[i], lru_b_x[i], lru_lambda[i], need_ctx)
        else:
            i = layer // 2
            y_lat, y_ctx = na_mixer(h_lat, h_ctx, na_w_qkv[i], na_w_out[i], na_rel_bias[i], need_ctx)
        x = x + g1 * y_lat
        x = x + g2 * sq_relu_mlp(modulate(x, norm2_g[layer], sh2, sc2), mlp_w_up[layer], mlp_w_down[layer])
        if need_ctx:
            ctx = ctx + cg1 * y_ctx
            ctx = ctx + cg2 * sq_relu_mlp(modulate(ctx, norm2_g[layer], csh2, csc2), mlp_w_up[layer], mlp_w_down[layer])
    return rms_norm(x, final_norm_g)
```

```python
import contextlib
import numpy as np
import concourse.bass as bass
import concourse.mybir as mybir
from concourse.bass_utils import run_bass_kernel_spmd

F32 = mybir.dt.float32
BF16 = mybir.dt.bfloat16
AF = mybir.ActivationFunctionType
ALU = mybir.AluOpType

D = 2048
KC = 16
DFF = 8192
NCTX = 256
NLAT = 4096
SEQ = NCTX + NLAT
NFULL = 2304
NOUT = NCTX + NFULL
NOWN = 2048
EPS = 1e-6
GRID_W = 64


class Buf:
    __slots__ = ("name", "w", "r", "multi")

    def __init__(self, name, multi=False):
        self.name = name
        self.w = {}
        self.r = {}
        self.multi = multi


class Sem:
    __slots__ = ("name", "h", "count", "nobarrier")

    def __init__(self, name):
        self.name = name
        self.h = None
        self.count = 0
        self.nobarrier = False


ENGS = ("pe", "act", "dve", "pool", "sp")
ENAME = {"pe": "tensor", "act": "scalar", "dve": "vector", "pool": "gpsimd", "sp": "sync"}


class Sched:
    def __init__(self, nc, stack):
        self.nc = nc
        self.streams = {e: [] for e in ENGS}
        self.esem = {}
        self.sems = []
        self.seen = {e: {} for e in ENGS}
        self.stack = stack
        for e in ENGS:
            self.esem[e] = self.new_sem("e_" + e)
        self.n = 0

    def new_sem(self, name):
        s = Sem(name)
        s.h = self.stack.enter_context(self.nc.semaphore(name))
        self.sems.append(s)
        return s

    def _deps(self, eng, reads, writes):
        waits = {}

        def add(d, skip_same):
            for s, (v, e) in d.items():
                if e == eng and (eng == "pe" or skip_same):
                    continue
                if waits.get(s, 0) < v:
                    waits[s] = v

        for b in reads:
            add(b.w, False)
        for b in writes:
            if not b.multi:
                add(b.w, False)
            add(b.r, False)
        out = []
        seen = self.seen[eng]
        for s, v in waits.items():
            if seen.get(s, 0) >= v:
                continue
            seen[s] = v
            out.append((s, v))
        return out

    def _commit(self, reads, writes, s, v, e):
        for b in reads:
            b.r[s] = (v, e)
        for b in writes:
            if b.multi:
                b.w[s] = (v, e)
            else:
                b.w = {s: (v, e)}
                b.r = {}

    def op(self, eng, fn, reads=(), writes=()):
        waits = self._deps(eng, reads, writes)
        s = self.esem[eng]
        s.count += 1
        self.streams[eng].append((waits, fn, s, 1))
        self._commit(reads, writes, s, s.count, eng)
        self.n += 1

    def dma(self, q, fn, sem, reads=(), writes=()):
        waits = self._deps(q, reads, writes)
        sem.count += 16
        self.streams[q].append((waits, fn, sem, 16))
        self._commit(reads, writes, sem, sem.count, "dma")
        self.n += 1

    def barrier(self):
        for e in ENGS:
            waits = []
            seen = self.seen[e]
            for s in self.sems:
                if s.nobarrier:
                    continue
                if s.count > 0 and seen.get(s, 0) < s.count:
                    seen[s] = s.count
                    waits.append((s, s.count))
            self.streams[e].append((waits, None, None, 0))

    def emit(self):
        nc = self.nc
        with nc.Block() as block:
            def make(engname):
                stream = self.streams[engname]

                def body(eng):
                    for (waits, fn, sem, inc) in stream:
                        for (s, v) in waits:
                            eng.wait_ge(s.h, v)
                        if fn is not None:
                            fn(eng).then_inc(sem.h, inc)
                return body

            for e in ENGS:
                getattr(block, ENAME[e])(make(e))
        self.streams = {e: [] for e in ENGS}


def pieces_of(T):
    n = (T + 511) // 512
    assert T % n == 0
    pc = T // n
    return [(i * pc, pc) for i in range(n)]


class Prog:
    def __init__(self, debug=()):
        self.debug = set(debug)
        self.nc = bass.Bass("TRN2", target_bir_lowering=False)
        self.top = contextlib.ExitStack()
        self.S = Sched(self.nc, self.top)
        self.ph = None
        self.inputs = {}
        self.free_sems = []
        self.ph_sems = []
        self.phase_no = 0

    def din(self, name, shape, dt=F32):
        t = self.nc.dram_tensor(name, list(shape), dt, kind="ExternalInput").ap()
        self.inputs[name] = (tuple(shape), dt)
        return t

    def dscr(self, name, shape, dt):
        kind = "ExternalOutput" if name in self.debug else "Internal"
        t = self.nc.dram_tensor(name, list(shape), dt, kind=kind).ap()
        return t, Buf(name, multi=True)

    def sb(self, name, shape, dt, top=False):
        st = self.top if top else self.ph
        name = f"s{self.phase_no}_{name}"
        t = st.enter_context(self.nc.sbuf_tensor(name, list(shape), dt))
        return t, Buf(name)

    def sbn(self, name, shape, dt, n):
        return [self.sb(f"{name}{i}", shape, dt) for i in range(n)]

    def sem(self, name, persistent=False):
        if persistent:
            return self.S.new_sem(name)
        if self.free_sems:
            s = self.free_sems.pop()
        else:
            s = self.S.new_sem(f"pool{len(self.S.sems)}")
        self.ph_sems.append(s)
        return s

    def phase_begin(self):
        self.phase_no += 1
        self.ph = contextlib.ExitStack()
        self.ph.__enter__()

    def phase_end(self):
        self.S.barrier()
        self.S.emit()
        self.ph.__exit__(None, None, None)
        self.ph = None
        self.free_sems.extend(self.ph_sems)
        self.ph_sems = []


class Rot:
    def __init__(self, items):
        self.items = items
        self.i = 0

    def next(self):
        it = self.items[self.i % len(self.items)]
        self.i += 1
        return it


L0_TILES = [(0, 640), (640, 640), (1280, 640), (1920, 640)]
L0_OTHER = [(2560, 896), (3456, 896)]
L1_TILES = [(256, 512), (768, 512), (1280, 512), (1792, 512)]
L1_KV_TILES = [(0, 256), (2304, 256)]
NEG = -30000.0


def na_lists():
    res = []
    for i in range(16):
        l = [(0, None), (1, None)]
        if i <= 1:
            for j in range(4):
                l.append((2 + j, i * 4 + j))
        else:
            for j in range(i - 2, i + 3):
                l.append((2 + j, 8 + (j - i + 2)))
        res.append(l)
    return res


def build(debug=(), upto=99):
    P = Prog(debug)
    nc, S = P.nc, P.S
    op, dma = S.op, S.dma

    xT_in = P.din("xT", [D, SEQ])
    cT_in = P.din("cT", [128, KC, 2])
    modw_in = [P.din(f"modw{l}", [96, 128, KC, 128]) for l in range(2)]
    modb_in = [P.din(f"modb{l}", [128, 96, 2]) for l in range(2)]
    ng_in = P.din("normg", [128, 5, KC])
    win_in = P.din("w_in", [49, 128, KC, 128])
    wout_in = P.din("w_out", [16, 128, KC, 128])
    wup_in = [P.din(f"w_up{l}", [64, 128, KC, 128]) for l in range(2)]
    wdn_in = [P.din(f"w_dn{l}", [16, 128, 64, 128]) for l in range(2)]
    wqkv_in = P.din("w_qkv", [48, 128, KC, 128])
    wno_in = P.din("w_no", [16, 128, KC, 128])
    rope_in = P.din("rope", [4, 128, SEQ])
    gup_in = P.din("gate_up", [17, 2, 512])
    umat_in = P.din("umat", [128, 6, 128])
    glag_in = P.din("gla_g", [128, 8])
    lruw_in = P.din("lru_w", [128, 32, 128])
    lrub_in = P.din("lru_b", [128, 32])
    lam_in = P.din("lru_lam", [128, 16])
    conv_in = P.din("conv", [128, 8, 6])
    nab_in = P.din("na_bias", [16, 128, 13, 128])
    out_d = P.nc.dram_tensor("out", [NOWN, D], F32, kind="ExternalOutput").ap()
    b_out = Buf("out", multi=True)
    b_in = Buf("inputs")

    wbf = {}

    def wscr(name, shape):
        wbf[name] = P.dscr("wbf_" + name, shape, BF16)

    for l in range(2):
        for part in range(6):
            wscr(f"mod{l}_{part}", [16, 128, KC, 128])
    wscr("w_in", [49, 128, KC, 128])
    wscr("w_out", [16, 128, KC, 128])
    for l in range(2):
        wscr(f"w_up{l}", [64, 128, KC, 128])
        wscr(f"w_dn{l}", [16, 128, 64, 128])
    wscr("w_qkv", [48, 128, KC, 128])
    wscr("w_no", [16, 128, KC, 128])

    xA, b_xA = P.dscr("xA", [D, NOUT], F32)
    xB, b_xB = P.dscr("xB", [D, NOUT], F32)
    qT, b_qT = P.dscr("qT", [NOUT // 256, 128, 4, 256], BF16)
    kT, b_kT = P.dscr("kT", [SEQ // 256, 128, 4, 256], BF16)
    vS, b_vS = P.dscr("vS", [SEQ // 256, 128, 2, 1024], BF16)
    LS, b_LS = P.dscr("LS", [2, SEQ // 256, 128, 2, 512], F32)
    ogT, b_ogT = P.dscr("ogT", [1024, NOUT], F32)
    xrT, b_xrT = P.dscr("xrT", [1024, SEQ], F32)
    gyT, b_gyT = P.dscr("gyT", [1024, NOUT], F32)
    mixT, b_mixT = P.dscr("mixT", [D, NOUT], BF16)
    q1T, b_q1T = P.dscr("q1T", [D, NOWN], BF16)
    k1T, b_k1T = P.dscr("k1T", [D, NOUT], BF16)
    v1S, b_v1S = P.dscr("v1S", [NOUT, D], BF16)
    at1T, b_at1T = P.dscr("at1T", [D, NOWN], BF16)

    ones_bf, b_ones = P.sb("ones_bf", [128, 128], BF16, top=True)
    ones1_bf, b_ones1 = P.sb("ones1_bf", [128, 128], BF16, top=True)
    ones256_bf, b_ones256 = P.sb("ones256_bf", [128, 128], BF16, top=True)
    ident_b, b_identb = P.sb("ident_b", [128, 128], BF16, top=True)
    umat, b_umat = P.sb("umat", [128, 6, 128], F32, top=True)
    silc, b_silc = P.sb("silc", [128, KC, 2], BF16, top=True)
    modv = [P.sb(f"modv{l}", [128, 96, 2], F32, top=True) for l in range(2)]
    gsc = [P.sb(f"gsc{l}", [128, 2, KC, 2], F32, top=True) for l in range(2)]
    ngt, b_ngt = P.sb("ngt", [128, 5, KC], F32, top=True)
    cst, b_cst = P.sb("cst", [128, 2], F32, top=True)
    ps = [P.top.enter_context(nc.psum_tensor(f"ps{i}", [128, 512], F32)) for i in range(8)]
    b_ps = [Buf(f"ps{i}") for i in range(8)]


    P.phase_begin()

    def conv(name, src):
        t, b = wbf[name]
        sw = P.sem("wc_" + name, persistent=True)
        sw.nobarrier = True
        dma("pool", lambda e: e.dma_start(out=t, in_=src), sw, reads=[b_in], writes=[b])

    def conv_mod(l, parts):
        for part in parts:
            conv(f"mod{l}_{part}", modw_in[l][part * 16:(part + 1) * 16])

    def conv_batch(k):
        if k == 0:
            conv_mod(0, [0, 1])
            conv("w_in", win_in)
        elif k == 1:
            conv_mod(0, [2])
            conv("w_out", wout_in)
            conv_mod(0, [3, 4, 5])
            conv("w_up0", wup_in[0])
            conv("w_dn0", wdn_in[0])
        elif k == 2:
            conv_mod(1, [0, 1, 2, 3, 4, 5])
        elif k == 3:
            conv("w_qkv", wqkv_in)
            conv("w_no", wno_in)
        elif k == 4:
            conv("w_up1", wup_in[1])
        else:
            conv("w_dn1", wdn_in[1])

    conv_batch(0)

    ctmp, b_ctmp = P.sb("ctmp", [128, KC, 2], F32)
    dma("sp", lambda e: e.dma_start(out=ctmp[:], in_=cT_in), P.sem("misc"), reads=[b_in], writes=[b_ctmp])
    dma("sp", lambda e: e.dma_start(out=ngt[:], in_=ng_in), P.sem("misc"), reads=[b_in], writes=[b_ngt])
    dma("sp", lambda e: e.dma_start(out=umat[:], in_=umat_in), P.sem("misc"), reads=[b_in], writes=[b_umat])
    op("act", lambda e: e.activation(out=silc[:], in_=ctmp[:], func=AF.Silu), reads=[b_ctmp], writes=[b_silc])
    op("dve", lambda e: e.memset(cst[:, 0:1], EPS), writes=[b_cst])
    op("dve", lambda e: e.memset(cst[:, 1:2], 1.0), writes=[b_cst])
    op("dve", lambda e: e.memset(ones_bf[:], 1.0 / D), writes=[b_ones])
    op("dve", lambda e: e.memset(ones1_bf[:], 1.0), writes=[b_ones1])
    op("dve", lambda e: e.memset(ones256_bf[:], 1.0 / 256), writes=[b_ones256])
    op("dve", lambda e: e.tensor_copy(out=ident_b[:], in_=umat[:, 4, :]), reads=[b_umat], writes=[b_identb])
    P.phase_end()

    def mod_parts(l, parts):
        P.phase_begin()
        mt, b_mt = modv[l]
        mb, b_mb = P.sb("mb", [128, 96, 2], F32)
        dma("sp", lambda e: e.dma_start(out=mb[:], in_=modb_in[l]), P.sem("misc"), reads=[b_in], writes=[b_mb])
        wts = [P.sb(f"mw{i}", [128, KC, 128], BF16) + (P.sem(f"mws{l}_{parts[0]}_{i}"),) for i in range(4)]
        jobs = [(part, n) for part in parts for n in range(16)]

        def load(idx):
            part, n = jobs[idx]
            w, b_w, sw = wts[idx % 4]
            wt_d, b_wd = wbf[f"mod{l}_{part}"]
            dma("sp", lambda e: e.dma_start(out=w[:], in_=wt_d[n]), sw, reads=[b_wd], writes=[b_w])

        for idx in range(min(3, len(jobs))):
            load(idx)
        for idx, (part, n) in enumerate(jobs):
            if idx + 3 < len(jobs):
                load(idx + 3)
            w, b_w, sw = wts[idx % 4]
            pst, b_pst = ps[part % 2], b_ps[part % 2]
            for kc in range(KC):
                op("pe", lambda e, w=w, kc=kc, n=n, pst=pst: e.matmul(
                    out=pst[:, 2 * n:2 * n + 2], lhsT=w[:, kc, :], rhs=silc[:, kc, :],
                    start=(kc == 0), stop=(kc == KC - 1)),
                   reads=[b_w, b_silc], writes=[b_pst])
            if n == 15:
                op("dve", lambda e, part=part, pst=pst: e.tensor_tensor(
                    out=mt[:, part * 16:(part + 1) * 16, :],
                    in0=pst[:, 0:32].rearrange("p (n j) -> p n j", j=2),
                    in1=mb[:, part * 16:(part + 1) * 16, :], op=ALU.add),
                   reads=[b_pst, b_mb], writes=[b_mt])
                if part in (1, 4):
                    ni = 0 if part == 1 else 1
                    g, b_g = gsc[l]
                    for j in range(2):
                        op("dve", lambda e, part=part, ni=ni, j=j, g=g: e.scalar_tensor_tensor(
                            out=g[:, ni, :, j], in0=mt[:, part * 16:(part + 1) * 16, j], scalar=1.0,
                            in1=ngt[:, 2 * l + ni, :], op0=ALU.add, op1=ALU.mult),
                           reads=[b_mt, b_ngt], writes=[b_g])
        P.phase_end()

    def sumsq_rstd(xsrc, b_xsrc, s0, T, xch, sq, rstd, b_rstd, pss, ones_t, b_ones_t, nch, row0=0):
        pcs = pieces_of(T)
        for c in range(nch):
            x_t, b_x, sx = xch.next()
            dma("sp", lambda e, x_t=x_t, c=c: e.dma_start(
                out=x_t[:, 0:T], in_=xsrc[row0 + c * 128:row0 + (c + 1) * 128, s0:s0 + T]),
                sx, reads=[b_xsrc], writes=[b_x])
            q_t, b_q = sq.next()
            op("act", lambda e, x_t=x_t, q_t=q_t: e.activation(out=q_t[:, 0:T], in_=x_t[:, 0:T], func=AF.Square),
               reads=[b_x], writes=[b_q])
            for pi, (c0, pc) in enumerate(pcs):
                pst, b_pst = pss[pi]
                op("pe", lambda e, pst=pst, q_t=q_t, c0=c0, pc=pc, c=c: e.matmul(
                    out=pst[:, 0:pc], lhsT=ones_t[:], rhs=q_t[:, c0:c0 + pc], start=(c == 0), stop=(c == nch - 1)),
                   reads=[b_ones_t, b_q], writes=[b_pst])
        for pi, (c0, pc) in enumerate(pcs):
            pst, b_pst = pss[pi]
            op("act", lambda e, pst=pst, c0=c0, pc=pc: e.activation(
                out=rstd[:, c0:c0 + pc], in_=pst[:, 0:pc], func=AF.Sqrt, bias=cst[:, 0:1]),
               reads=[b_pst, b_cst], writes=[b_rstd])
            op("dve", lambda e, c0=c0, pc=pc: e.reciprocal(out=rstd[:, c0:c0 + pc], in_=rstd[:, c0:c0 + pc]),
               reads=[b_rstd], writes=[b_rstd])

    def norm_tile(xsrc, b_xsrc, s0, T, ranges, hT, b_hT, xch, sq, rstd, b_rstd, pss):
        sumsq_rstd(xsrc, b_xsrc, s0, T, xch, sq, rstd, b_rstd, pss, ones_bf, b_ones, KC)
        for c in range(KC):
            x_t, b_x, sx = xch.next()
            dma("sp", lambda e, x_t=x_t, c=c: e.dma_start(out=x_t[:, 0:T], in_=xsrc[c * 128:(c + 1) * 128, s0:s0 + T]),
                sx, reads=[b_xsrc], writes=[b_x])
            for (c0, c1, gfn, sfn) in ranges:
                op("dve", lambda e, x_t=x_t, c0=c0, c1=c1, gfn=gfn, c=c: e.scalar_tensor_tensor(
                    out=x_t[:, c0:c1], in0=x_t[:, c0:c1], scalar=gfn(c), in1=rstd[:, c0:c1],
                    op0=ALU.mult, op1=ALU.mult),
                   reads=[b_x, b_rstd], writes=[b_x])
                op("act", lambda e, x_t=x_t, c0=c0, c1=c1, sfn=sfn, c=c: e.activation(
                    out=hT[:, c, c0:c1], in_=x_t[:, c0:c1], func=AF.Identity, bias=sfn(c)),
                   reads=[b_x], writes=[b_hT])

    def mod_ranges(l, ni, s0, T):
        g, _ = gsc[l]
        mt, _ = modv[l]
        shp = 0 if ni == 0 else 3
        res = []
        a, b = s0, s0 + T
        if a < NCTX:
            res.append((0, min(b, NCTX) - a, (lambda c: g[:, ni, c, 1:2]), (lambda c: mt[:, shp * 16 + c, 1:2])))
        if b > NCTX:
            res.append((max(a, NCTX) - a, b - a, (lambda c: g[:, ni, c, 0:1]), (lambda c: mt[:, shp * 16 + c, 0:1])))
        return res

    def gate_ranges(l, gpart, s0, T):
        mt, _ = modv[l]
        res = []
        a, b = s0, s0 + T
        if a < NCTX:
            res.append((0, min(b, NCTX) - a, (lambda n: mt[:, gpart * 16 + n, 1:2])))
        if b > NCTX:
            res.append((max(a, NCTX) - a, b - a, (lambda n: mt[:, gpart * 16 + n, 0:1])))
        return res

    class WPool:
        def __init__(self, bufs):
            self.bufs = bufs
            self.nw = len(bufs)
            self.pos = 0
            self.q = []

        def load(self, key):
            wname, n = key
            wt_d, b_wd = wbf[wname]
            w, b_w, sw = self.bufs[self.pos % self.nw]
            self.pos += 1
            dma("sp", lambda e: e.dma_start(out=w[:], in_=wt_d[n]), sw, reads=[b_wd], writes=[b_w])
            self.q.append((key, w, b_w))

        def pop(self, key):
            k, w, b_w = self.q.pop(0)
            assert k == key, (k, key)
            return w, b_w

    def lin(wname, kcn, groups, act, b_act, T, epi, wts, psr, msl=None, nxt=()):
        pcs = pieces_of(T)
        flat = [(wname, n) for g in groups for n in g]
        upcoming = flat + list(nxt)
        ahead = wts.nw - 1
        npre = len(wts.q)
        assert [k for (k, _, _) in wts.q] == upcoming[:npre], ([k for (k, _, _) in wts.q], upcoming[:npre])
        state = {"issued": npre}

        def topup(consumed):
            while state["issued"] < len(upcoming) and state["issued"] < consumed + ahead:
                wts.load(upcoming[state["issued"]])
                state["issued"] += 1

        idx = 0
        topup(1)
        for g in groups:
            pls = []
            for n in g:
                topup(idx + 1)
                w, b_w = wts.pop((wname, n))
                idx += 1
                pl = [psr.next() for _ in pcs]
                for kc in range(kcn):
                    for (pst, b_pst), (c0, pc) in zip(pl, pcs):
                        if msl is None:
                            lhs = w[:, kc, :]
                            o = pst[:, 0:pc]
                        else:
                            lhs = w[:, kc, msl[0]:msl[1]]
                            o = pst[0:msl[1] - msl[0], 0:pc]
                        op("pe", lambda e, o=o, lhs=lhs, kc=kc, c0=c0, pc=pc: e.matmul(
                            out=o, lhsT=lhs, rhs=act[:, kc, c0:c0 + pc],
                            start=(kc == 0), stop=(kc == kcn - 1)),
                           reads=[b_w, b_act], writes=[b_pst])
                pls.append([(pst, b_pst, c0, pc) for (pst, b_pst), (c0, pc) in zip(pl, pcs)])
            epi(g, pls)
        topup(idx + 1)

    def lin_tok(wname, g0, ngroups, act, b_act, T, epi, wtk, psr):
        wt_d, b_wd = wbf[wname]
        for gi in range(ngroups):
            w, b_w, sw = wtk.next()
            dma("sp", lambda e, w=w, gi=gi: e.dma_start(
                out=w[:], in_=wt_d[g0 + gi * 4:g0 + gi * 4 + 4].rearrange("g p k j -> p g k j")),
                sw, reads=[b_wd], writes=[b_w])
            for tb in range(T // 128):
                pst, b_pst = psr.next()
                for kc in range(KC):
                    op("pe", lambda e, pst=pst, w=w, kc=kc, tb=tb: e.matmul(
                        out=pst[:, :].rearrange("p (g j) -> p g j", j=128), lhsT=act[:, kc, tb * 128:(tb + 1) * 128],
                        rhs=w[:, :, kc, :], start=(kc == 0), stop=(kc == KC - 1)),
                       reads=[b_w, b_act], writes=[b_pst])
                epi(gi, tb, pst, b_pst)

    def resid_epi(l, gpart, s0, T, xsrc, b_xsrc, xdst, b_xdst, xo, d0=None):
        granges = gate_ranges(l, gpart, s0, T)
        if d0 is None:
            d0 = s0

        def epi(g, pls):
            n = g[0]
            pl = pls[0]
            x_t, b_x, sx = xo.next()
            dma("sp", lambda e: e.dma_start(out=x_t[:, 0:T], in_=xsrc[n * 128:(n + 1) * 128, s0:s0 + T]),
                sx, reads=[b_xsrc], writes=[b_x])
            for (pst, b_pst, c0, pc) in pl:
                for (r0, r1, gfn) in granges:
                    lo, hi = max(r0, c0), min(r1, c0 + pc)
                    if lo >= hi:
                        continue
                    op("dve", lambda e, pst=pst, lo=lo, hi=hi, c0=c0, gfn=gfn: e.scalar_tensor_tensor(
                        out=x_t[:, lo:hi], in0=pst[:, lo - c0:hi - c0], scalar=gfn(n), in1=x_t[:, lo:hi],
                        op0=ALU.mult, op1=ALU.add),
                       reads=[b_pst, b_x], writes=[b_x])
            dma("sp", lambda e: e.dma_start(out=xdst[n * 128:(n + 1) * 128, d0:d0 + T], in_=x_t[:, 0:T]),
                sx, reads=[b_x], writes=[b_xdst])
        return epi

    def mlp_phase(l, xsrc, b_xsrc, xdst, b_xdst, tiles):
        P.phase_begin()
        TM = max(t for _, t in tiles)
        if l == 0:
            conv_batch(5)
        hTs = [P.sb(f"hT{i}", [128, KC, TM], BF16) for i in range(2)]
        aT, b_aT = P.sb("aT", [128, 64, TM], BF16)
        rstds = [P.sb(f"rstd{i}", [128, TM], F32) for i in range(2)]
        xch = Rot([P.sb(f"xch{i}", [128, TM], F32) + (P.sem(f"mlp{l}_x{i}"),) for i in range(4)])
        sq = Rot(P.sbn("sq", [128, TM], BF16, 2))
        wup = WPool([P.sb(f"wup{i}", [128, KC, 128], BF16) + (P.sem(f"mlp{l}_wu{i}"),) for i in range(3)])
        wdn = WPool([P.sb(f"wdn{i}", [128, 64, 128], BF16) + (P.sem(f"mlp{l}_wd{i}"),) for i in range(2)])
        rl = Rot(P.sbn("rl", [128, TM], F32, 2))
        psr = Rot(list(zip(ps, b_ps)))
        def do_norm(i):
            s0, T = tiles[i]
            hT, b_hT = hTs[i % 2]
            rstd, b_rstd = rstds[i % 2]
            norm_tile(xsrc, b_xsrc, s0, T, mod_ranges(l, 1, s0, T), hT, b_hT, xch, sq, rstd, b_rstd,
                      [psr.next(), psr.next()])

        do_norm(0)
        for ti, (s0, T) in enumerate(tiles):
            if ti + 1 < len(tiles):
                do_norm(ti + 1)
            hT, b_hT = hTs[ti % 2]

            def epi_up(g, pls, T=T):
                n = g[0]
                r_t, b_r = rl.next()
                for (pst, b_pst, c0, pc) in pls[0]:
                    op("act", lambda e, pst=pst, c0=c0, pc=pc: e.activation(
                        out=r_t[:, c0:c0 + pc], in_=pst[:, 0:pc], func=AF.Relu), reads=[b_pst], writes=[b_r])
                op("dve", lambda e: e.tensor_tensor(
                    out=aT[:, n, 0:T], in0=r_t[:, 0:T], in1=r_t[:, 0:T], op=ALU.mult), reads=[b_r], writes=[b_aT])

            last = ti + 1 == len(tiles)
            if ti == 0:
                wdn.load((f"w_dn{l}", 0))
            lin(f"w_up{l}", KC, [[n] for n in range(64)], hT, b_hT, T, epi_up, wup, psr,
                nxt=[] if last else [(f"w_up{l}", n) for n in range(2)])
            lin(f"w_dn{l}", 64, [[n] for n in range(16)], aT, b_aT, T,
                resid_epi(l, 5, s0, T, xsrc, b_xsrc, xdst, b_xdst, xch), wdn, psr,
                nxt=[] if last else [(f"w_dn{l}", 0)])
        P.phase_end()

    def proj_resid_phase(tag, l, wname, asrc, b_asrc, a0_of, xsrc, b_xsrc, xdst, b_xdst, tiles):
        P.phase_begin()
        TM = max(t for _, t in tiles)
        if tag == "l0d":
            conv_batch(4)
        aTt = Rot([P.sb(f"pa{i}", [128, KC, TM], BF16) + (P.sem(f"{tag}_a{i}"),) for i in range(2)])
        xo = Rot([P.sb(f"pxo{i}", [128, TM], F32) + (P.sem(f"{tag}_x{i}"),) for i in range(4)])
        wts = WPool([P.sb(f"pw{i}", [128, KC, 128], BF16) + (P.sem(f"{tag}_w{i}"),) for i in range(3)])
        psr = Rot(list(zip(ps, b_ps)))
        for ti, (s0, T) in enumerate(tiles):
            a_t, b_a, sa = aTt.next()
            a0 = a0_of(s0)
            dma("sp", lambda e, a_t=a_t, a0=a0, T=T: e.dma_start(
                out=a_t[:, :, 0:T], in_=asrc[:, a0:a0 + T].rearrange("(c p) t -> p c t", p=128)),
                sa, reads=[b_asrc], writes=[b_a])
            lin(wname, KC, [[n] for n in range(16)], a_t, b_a, T,
                resid_epi(l, 2, s0, T, xsrc, b_xsrc, xdst, b_xdst, xo), wts, psr,
                nxt=[] if ti + 1 == len(tiles) else [(wname, n) for n in range(2)])
        P.phase_end()

    mod_parts(0, [0, 1])

    def l0a():
        P.phase_begin()
        TM = 896
        hTs = [P.sb(f"hT{i}", [128, KC, TM], BF16) for i in range(2)]
        rstds = [P.sb(f"rstd{i}", [128, TM], F32) for i in range(2)]
        xch = Rot([P.sb(f"xch{i}", [128, TM], F32) + (P.sem(f"a_x{i}"),) for i in range(3)])
        sq = Rot(P.sbn("sq", [128, TM], BF16, 2))
        wts = WPool([P.sb(f"w{i}", [128, KC, 128], BF16) + (P.sem(f"a_w{i}"),) for i in range(4)])
        wtk = Rot([P.sb(f"wk{i}", [128, 4, KC, 128], BF16) + (P.sem(f"a_wk{i}"),) for i in range(2)])
        assert True
        rope = [P.sb(f"rope{i}", [128, TM], F32) + (P.sem(f"a_rope{i}"),) for i in range(4)]
        ob = Rot([P.sb(f"ob{i}", [128, TM], F32) + (P.sem(f"a_ob{i}"),) for i in range(3)])
        obh = Rot([P.sb(f"obh{i}", [128, TM], BF16) + (P.sem(f"a_obh{i}"),) for i in range(3)])
        t1r = Rot(P.sbn("t1_", [128, TM], F32, 1))
        t2r = Rot(P.sbn("t2_", [128, TM], F32, 1))
        vob = Rot([P.sb(f"vob{i}", [128, 512], BF16) + (P.sem(f"a_vob{i}"),) for i in range(3)])
        glw = [P.sb(f"glw{i}", [17, TM], F32) for i in range(2)]
        gup, b_gup = P.sb("gup", [17, 2, 512], F32)
        et = Rot(P.sbn("et", [128, 512], F32, 2))
        lt = Rot([P.sb(f"lt{i}", [128, 512], F32) + (P.sem(f"a_lt{i}"),) for i in range(3)])
        psr = Rot(list(zip(ps, b_ps)))
        dma("sp", lambda e: e.dma_start(out=gup[:], in_=gup_in), P.sem("misc"), reads=[b_in], writes=[b_gup])
        for (g_t, b_g) in glw:
            op("dve", lambda e, g_t=g_t: e.memset(g_t[:], 1.0), writes=[b_g])

        def do_norm(s0, T, slot):
            hT, b_hT = hTs[slot]
            rstd, b_rstd = rstds[slot]
            norm_tile(xT_in, b_in, s0, T, mod_ranges(0, 0, s0, T), hT, b_hT, xch, sq, rstd, b_rstd,
                      [psr.next(), psr.next()])

        def do_tile(s0, T, slot, nxt_tile):
            full = s0 < NOUT
            hT, b_hT = hTs[slot]
            for i in range(4):
                if i < 2 and not full:
                    continue
                r_t, b_r, s_r = rope[i]
                dma("sp", lambda e, r_t=r_t, i=i: e.dma_start(out=r_t[:, 0:T], in_=rope_in[i, :, s0:s0 + T]),
                    s_r, reads=[b_in], writes=[b_r])

            def rope_epi(dst, b_dst, ci, si, base):
                (c_t, b_c, _), (s_t, b_s, _) = rope[ci], rope[si]

                def epi(g, pls):
                    h = g[0] - base
                    o_t, b_o, so = obh.next()
                    t1, b_t1 = t1r.next()
                    t2, b_t2 = t2r.next()
                    for (pa, b_pa, c0, pc), (pb, b_pb, _, _) in zip(pls[0], pls[1]):
                        op("dve", lambda e, pa=pa, c0=c0, pc=pc: e.tensor_tensor(
                            out=t1[:, c0:c0 + pc], in0=pa[:, 0:pc], in1=c_t[:, c0:c0 + pc], op=ALU.mult),
                           reads=[b_pa, b_c], writes=[b_t1])
                        op("dve", lambda e, pb=pb, c0=c0, pc=pc: e.tensor_tensor(
                            out=t2[:, c0:c0 + pc], in0=pb[:, 0:pc], in1=s_t[:, c0:c0 + pc], op=ALU.mult),
                           reads=[b_pb, b_s], writes=[b_t2])
                    op("dve", lambda e: e.tensor_tensor(out=o_t[:, 0:T], in0=t1[:, 0:T], in1=t2[:, 0:T], op=ALU.add),
                       reads=[b_t1, b_t2], writes=[b_o])
                    for sp in range(s0 // 256, (s0 + T - 1) // 256 + 1):
                        lo, hi = max(s0, sp * 256), min(s0 + T, sp * 256 + 256)
                        dma("sp", lambda e, sp=sp, lo=lo, hi=hi: e.dma_start(
                            out=dst[sp, :, h, lo - sp * 256:hi - sp * 256], in_=o_t[:, lo - s0:hi - s0]),
                            so, reads=[b_o], writes=[b_dst])
                return epi

            def act_epi(dst, b_dst, base, func, scale=1.0):
                def epi(g, pls):
                    n = g[0] - base
                    o_t, b_o, so = ob.next()
                    for (pst, b_pst, c0, pc) in pls[0]:
                        op("act", lambda e, pst=pst, c0=c0, pc=pc: e.activation(
                            out=o_t[:, c0:c0 + pc], in_=pst[:, 0:pc], func=func, scale=scale),
                           reads=[b_pst], writes=[b_o])
                    dma("sp", lambda e: e.dma_start(out=dst[n * 128:(n + 1) * 128, s0:s0 + T], in_=o_t[:, 0:T]),
                        so, reads=[b_o], writes=[b_dst])
                return epi

            def v_epi(gi, tb, pst, b_pst):
                o_t, b_o, so = vob.next()
                op("act", lambda e: e.activation(out=o_t[:], in_=pst[:], func=AF.Copy), reads=[b_pst], writes=[b_o])
                tok = s0 + tb * 128
                dma("sp", lambda e: e.dma_start(
                    out=vS[tok // 256, :, (tok // 128) % 2, gi * 512:(gi + 1) * 512], in_=o_t[:]),
                    so, reads=[b_o], writes=[b_vS])


            seq = []
            if full:
                seq.append(dict(groups=[[h, 4 + h] for h in range(4)], epi=rope_epi(qT, b_qT, 0, 1, 0)))
            seq.append(dict(groups=[[8 + h, 12 + h] for h in range(4)], epi=rope_epi(kT, b_kT, 2, 3, 8)))
            seq.append("v")
            if full:
                seq.append(dict(groups=[[24 + n] for n in range(8)], epi=act_epi(ogT, b_ogT, 24, AF.Silu)))
            seq.append(dict(groups=[[32 + n] for n in range(8)], epi=act_epi(xrT, b_xrT, 32, AF.Copy)))
            if full:
                seq.append(dict(groups=[[40 + n] for n in range(8)], epi=act_epi(gyT, b_gyT, 40, AF.Gelu)))
            for di in range(2):
                g_t, b_g = glw[di]

                def gl_epi(g, pls, g_t=g_t, b_g=b_g):
                    for (pst, b_pst, c0, pc) in pls[0]:
                        op("act", lambda e, pst=pst, c0=c0, pc=pc, g_t=g_t: e.activation(
                            out=g_t[0:16, c0:c0 + pc], in_=pst[0:16, 0:pc], func=AF.Copy),
                           reads=[b_pst], writes=[b_g])

                seq.append(dict(groups=[[48]], epi=gl_epi, msl=(16 * di, 16 * di + 16)))
            for i, cdef in enumerate(seq):
                if cdef == "v":
                    lin_tok("w_in", 16, 2, hT, b_hT, T, v_epi, wtk, psr)
                    continue
                nk = list(nxt_tile)
                for later in seq[i + 1:]:
                    if later != "v":
                        nk = [("w_in", n) for g in later["groups"] for n in g][:3]
                        break
                lin("w_in", KC, cdef["groups"], hT, b_hT, T, cdef["epi"], wts, psr, msl=cdef.get("msl"), nxt=nk)
            for di in range(2):
                g_t, b_g = glw[di]
                for tb in range(T // 128):
                    pst, b_pst = psr.next()
                    op("pe", lambda e, pst=pst, g_t=g_t, tb=tb, di=di: e.matmul(
                        out=pst[:, :], lhsT=g_t[0:17, tb * 128:(tb + 1) * 128], rhs=gup[0:17, di, :],
                        start=True, stop=True), reads=[b_g, b_gup], writes=[b_pst])
                    e_t, b_e = et.next()
                    l_t, b_l, sl = lt.next()
                    op("act", lambda e, pst=pst, e_t=e_t: e.activation(out=e_t[:], in_=pst[:], func=AF.Exp, scale=-1.0),
                       reads=[b_pst], writes=[b_e])
                    op("act", lambda e, l_t=l_t, e_t=e_t: e.activation(out=l_t[:], in_=e_t[:], func=AF.Ln, bias=cst[:, 1:2]),
                       reads=[b_e], writes=[b_l])
                    tok = s0 + tb * 128
                    dma("sp", lambda e, l_t=l_t, tok=tok, di=di: e.dma_start(
                        out=LS[di, tok // 256, :, (tok // 128) % 2, :], in_=l_t[:]),
                        sl, reads=[b_l], writes=[b_LS])

        tl = L0_TILES + L0_OTHER
        do_norm(tl[0][0], tl[0][1], 0)
        for ti, (s0_, T_) in enumerate(tl):
            if ti + 1 < len(tl):
                do_norm(tl[ti + 1][0], tl[ti + 1][1], (ti + 1) % 2)
            if ti + 1 == len(tl):
                nt = []
            elif tl[ti + 1][0] < NOUT:
                nt = [("w_in", 0), ("w_in", 4), ("w_in", 1)]
            else:
                nt = [("w_in", 8), ("w_in", 12), ("w_in", 9)]
            do_tile(s0_, T_, ti % 2, nt)
        P.phase_end()

    if upto >= 1:
        l0a()

    def l0b():
        P.phase_begin()
        conv_batch(1)
        conv_batch(2)
        oacc, b_oacc = P.sb("oacc", [128, 8, NOUT], F32)
        b_oaccs = [Buf(f"oacc_blk{i}") for i in range(NOUT // 128)]
        for i in range(NOUT // 128):
            op("dve", lambda e, i=i: e.memset(oacc[:, :, i * 128:(i + 1) * 128], 0.0), writes=[b_oaccs[i]])
        Sst = [[P.sb(f"S{d}{h}", [128, 256], F32) for h in range(4)] for d in range(2)]
        Sbf = [[P.sb(f"Sb{d}{h}", [128, 256], BF16) for h in range(4)] for d in range(2)]
        qsp = Rot([P.sb(f"qsp{i}", [128, 4, 256], BF16) + (P.sem(f"b_q{i}"),) for i in range(6)])
        ksp = Rot([P.sb(f"ksp{i}", [128, 4, 256], BF16) + (P.sem(f"b_k{i}"),) for i in range(6)])
        vsp = Rot([P.sb(f"vsp{i}", [128, 2, 1024], BF16) + (P.sem(f"b_v{i}"),) for i in range(6)])
        lsp = Rot([P.sb(f"lsp{i}", [128, 2, 512], F32) + (P.sem(f"b_l{i}"),) for i in range(6)])
        NR = 6
        eb_r = Rot(P.sbn("eb", [128, 128], F32, NR))
        enb_r = Rot(P.sbn("enb", [128, 128], F32, NR))
        kt_r = Rot(P.sbn("ktl", [128, 128], BF16, NR))
        qt_r = Rot(P.sbn("qtl", [128, 128], BF16, NR))
        kh_r = Rot(P.sbn("khT", [128, 128], BF16, NR))
        khs_r = Rot(P.sbn("khs", [128, 128], BF16, NR))
        am_r = Rot(P.sbn("attm", [128, 128], BF16, NR))
        psAA = [[(ps[4 * d], b_ps[4 * d]), (ps[4 * d + 1], b_ps[4 * d + 1])] for d in range(2)]
        psO = [(ps[4 * d + 2], b_ps[4 * d + 2]) for d in range(2)]
        psD = [(ps[4 * d + 3], b_ps[4 * d + 3]) for d in range(2)]
        for d in range(2):
            for h in range(4):
                s_t, b_s = Sst[d][h]
                sb_t, b_sb = Sbf[d][h]
                op("dve", lambda e, s_t=s_t: e.memset(s_t[:], 0.0), writes=[b_s])
                op("dve", lambda e, sb_t=sb_t: e.memset(sb_t[:], 0.0), writes=[b_sb])

        def spans(d):
            if d == 0:
                return [(0, True)] + [(NCTX + 256 * i, True) for i in range(9)]
            return [(0, True)] + [(NCTX + 256 * i, i < 9) for i in range(15, -1, -1)]

        def unit_front(d, uidx, full, blk, h, t0, k_t, b_k, v_t, b_v, l_t, b_l, q_t, b_q):
            pa, b_pa = psAA[d][uidx % 2]
            eb, b_eb = eb_r.next()
            enb, b_enb = enb_r.next()
            ktl, b_ktl = kt_r.next()
            khT, b_khT = kh_r.next()
            khs, b_khs = khs_r.next()
            c = dict(d=d, full=full, blk=blk, h=h, t0=t0, v_t=v_t, b_v=b_v, eb=eb, b_eb=b_eb, khs=khs, b_khs=b_khs)
            op("pe", lambda e: e.matmul(
                out=pa[:, 0:128], lhsT=l_t[:, blk, h * 128:(h + 1) * 128], rhs=umat[:, d, :],
                start=True, stop=True), reads=[b_l, b_umat], writes=[b_pa])
            op("act", lambda e: e.activation(out=eb[:], in_=pa[:, 0:128], func=AF.Exp),
               reads=[b_pa], writes=[b_eb])
            op("act", lambda e: e.activation(out=enb[:], in_=pa[:, 0:128], func=AF.Exp, scale=-1.0),
               reads=[b_pa], writes=[b_enb])
            op("dve", lambda e: e.tensor_tensor(
                out=ktl[:], in0=k_t[:, h, blk * 128:(blk + 1) * 128], in1=enb[:], op=ALU.mult),
               reads=[b_k, b_enb], writes=[b_ktl])
            for cc in range(2):
                lc = cc * 64 + (63 if d == 0 else 0)
                op("act", lambda e, cc=cc, lc=lc: e.activation(
                    out=khT[:, cc * 64:(cc + 1) * 64], in_=ktl[:, cc * 64:(cc + 1) * 64],
                    func=AF.Copy, scale=eb[:, lc:lc + 1]),
                   reads=[b_ktl, b_eb], writes=[b_khT])
            op("pe", lambda e: e.matmul(
                out=pa[:, 128:256], lhsT=khT[:], rhs=ident_b[:], start=True, stop=True),
               reads=[b_khT, b_identb], writes=[b_pa])
            op("act", lambda e: e.activation(out=khs[:], in_=pa[:, 128:256], func=AF.Copy),
               reads=[b_pa], writes=[b_khs])
            if full:
                qtl, b_qtl = qt_r.next()
                attm, b_am = am_r.next()
                c.update(qtl=qtl, b_qtl=b_qtl, attm=attm, b_am=b_am)
                op("dve", lambda e: e.tensor_tensor(
                    out=qtl[:], in0=q_t[:, h, blk * 128:(blk + 1) * 128], in1=eb[:], op=ALU.mult),
                   reads=[b_q, b_eb], writes=[b_qtl])
                op("pe", lambda e: e.matmul(
                    out=pa[:, 256:384], lhsT=ktl[:], rhs=qtl[:], start=True, stop=True),
                   reads=[b_ktl, b_qtl], writes=[b_pa])
                op("dve", lambda e: e.tensor_tensor(
                    out=attm[:], in0=pa[:, 256:384], in1=umat[:, 2 + d, :], op=ALU.mult),
                   reads=[b_pa, b_umat], writes=[b_am])
            return c

        def unit_back(c):
            d, full, blk, h, t0 = c["d"], c["full"], c["blk"], c["h"], c["t0"]
            v_t, b_v, eb, b_eb, khs, b_khs = c["v_t"], c["b_v"], c["eb"], c["b_eb"], c["khs"], c["b_khs"]
            po, b_po = psO[d]
            pd, b_pd = psD[d]
            s_t, b_s = Sst[d][h]
            sb_t, b_sb = Sbf[d][h]
            if full:
                qtl, b_qtl, attm, b_am = c["qtl"], c["b_qtl"], c["attm"], c["b_am"]
                for j in range(2):
                    op("pe", lambda e, j=j: e.matmul(
                        out=po[:, j * 128:(j + 1) * 128],
                        lhsT=v_t[:, blk, h * 256 + j * 128:h * 256 + (j + 1) * 128], rhs=attm[:],
                        start=True, stop=True), reads=[b_v, b_am], writes=[b_po])
            corder = [0, 1] if d == 0 else [1, 0]
            for ci, cc in enumerate(corder):
                lc = cc * 64 + (63 if d == 0 else 0)
                if full:
                    for j in range(2):
                        op("pe", lambda e, j=j, cc=cc: e.matmul(
                            out=po[:, 256 + j * 128 + cc * 64:256 + j * 128 + (cc + 1) * 64],
                            lhsT=sb_t[:, j * 128:(j + 1) * 128], rhs=qtl[:, cc * 64:(cc + 1) * 64],
                            start=True, stop=True), reads=[b_sb, b_qtl], writes=[b_po])
                op("pe", lambda e, cc=cc: e.matmul(
                    out=pd[:, 0:256], lhsT=khs[cc * 64:(cc + 1) * 64, :],
                    rhs=v_t[cc * 64:(cc + 1) * 64, blk, h * 256:(h + 1) * 256], start=True, stop=True),
                   reads=[b_khs, b_v], writes=[b_pd])
                op("dve", lambda e, lc=lc: e.scalar_tensor_tensor(
                    out=s_t[:], in0=s_t[:], scalar=eb[:, lc:lc + 1], in1=pd[:, 0:256],
                    op0=ALU.mult, op1=ALU.add), reads=[b_s, b_eb, b_pd], writes=[b_s])
                op("act", lambda e: e.activation(out=sb_t[:], in_=s_t[:], func=AF.Copy),
                   reads=[b_s], writes=[b_sb])
            if full:
                b_oa = b_oaccs[t0 // 128]
                for j in range(2):
                    dst = oacc[:, h * 2 + j, t0:t0 + 128]
                    for part in range(2):
                        op("dve", lambda e, dst=dst, j=j, part=part: e.tensor_tensor(
                            out=dst, in0=dst, in1=po[:, part * 256 + j * 128:part * 256 + (j + 1) * 128], op=ALU.add),
                           reads=[b_po, b_oa], writes=[b_oa])

        def dir_gen(d):
            prev = None
            uidx = 0
            for (sp0, full) in spans(d):
                k_t, b_k, sk = ksp.next()
                dma("sp", lambda e, k_t=k_t, sp0=sp0: e.dma_start(out=k_t[:], in_=kT[sp0 // 256]),
                    sk, reads=[b_kT], writes=[b_k])
                v_t, b_v, sv = vsp.next()
                dma("sp", lambda e, v_t=v_t, sp0=sp0: e.dma_start(out=v_t[:], in_=vS[sp0 // 256]),
                    sv, reads=[b_vS], writes=[b_v])
                l_t, b_l, sl = lsp.next()
                dma("sp", lambda e, l_t=l_t, sp0=sp0: e.dma_start(out=l_t[:], in_=LS[d, sp0 // 256]),
                    sl, reads=[b_LS], writes=[b_l])
                q_t = b_q = None
                if full:
                    q_t, b_q, sq_ = qsp.next()
                    dma("sp", lambda e, q_t=q_t, sp0=sp0: e.dma_start(out=q_t[:], in_=qT[sp0 // 256]),
                        sq_, reads=[b_qT], writes=[b_q])
                for blk in ([0, 1] if d == 0 else [1, 0]):
                    t0 = sp0 + blk * 128
                    for h in range(4):
                        c = unit_front(d, uidx, full, blk, h, t0, k_t, b_k, v_t, b_v, l_t, b_l, q_t, b_q)
                        uidx += 1
                        if prev is not None:
                            unit_back(prev)
                            yield
                        prev = c
            unit_back(prev)
            yield

        gens = [dir_gen(0), dir_gen(1)]
        while gens:
            for g_ in list(gens):
                try:
                    next(g_)
                except StopIteration:
                    gens.remove(g_)

        glag, b_glag = P.sb("glag", [128, 8], F32)
        dma("sp", lambda e: e.dma_start(out=glag[:], in_=glag_in), P.sem("misc"), reads=[b_in], writes=[b_glag])
        sqr = Rot(P.sbn("gsq", [128, 512], BF16, 3))
        rs_r = Rot(P.sbn("grs", [128, 512], F32, 2))
        sg_r = Rot([P.sb(f"gsg{i}", [128, 512], F32) + (P.sem(f"b_sg{i}"),) for i in range(3)])
        tm_r = Rot(P.sbn("gtm", [128, 512], F32, 2))
        go_r = Rot([P.sb(f"ggo{i}", [128, 512], BF16) + (P.sem(f"b_go{i}"),) for i in range(3)])
        psr = Rot(list(zip(ps, b_ps)))
        for h in range(4):
            for c0 in range(0, NOUT, 512):
                pst, b_pst = psr.next()
                for j in range(2):
                    q_t, b_q = sqr.next()
                    op("act", lambda e, q_t=q_t, h=h, j=j, c0=c0: e.activation(
                        out=q_t[:], in_=oacc[:, h * 2 + j, c0:c0 + 512], func=AF.Square),
                       reads=b_oaccs[c0 // 128:c0 // 128 + 4], writes=[b_q])
                    op("pe", lambda e, pst=pst, q_t=q_t, j=j: e.matmul(
                        out=pst[:], lhsT=ones256_bf[:], rhs=q_t[:], start=(j == 0), stop=(j == 1)),
                       reads=[b_ones256, b_q], writes=[b_pst])
                rs, b_rs = rs_r.next()
                op("act", lambda e, rs=rs, pst=pst: e.activation(out=rs[:], in_=pst[:], func=AF.Sqrt, bias=cst[:, 0:1]),
                   reads=[b_pst, b_cst], writes=[b_rs])
                op("dve", lambda e, rs=rs: e.reciprocal(out=rs[:], in_=rs[:]), reads=[b_rs], writes=[b_rs])
                for j in range(2):
                    n = h * 2 + j
                    sg, b_sg, ssg = sg_r.next()
                    dma("sp", lambda e, sg=sg, n=n, c0=c0: e.dma_start(out=sg[:], in_=ogT[n * 128:(n + 1) * 128, c0:c0 + 512]),
                        ssg, reads=[b_ogT], writes=[b_sg])
                    tm, b_tm = tm_r.next()
                    go, b_go, sgo = go_r.next()
                    op("dve", lambda e, tm=tm, n=n, c0=c0, rs=rs: e.scalar_tensor_tensor(
                        out=tm[:], in0=oacc[:, n, c0:c0 + 512], scalar=glag[:, n:n + 1], in1=rs[:],
                        op0=ALU.mult, op1=ALU.mult), reads=b_oaccs[c0 // 128:c0 // 128 + 4] + [b_glag, b_rs], writes=[b_tm])
                    op("dve", lambda e, go=go, tm=tm, sg=sg: e.tensor_tensor(out=go[:], in0=tm[:], in1=sg[:], op=ALU.mult),
                       reads=[b_tm, b_sg], writes=[b_go])
                    dma("sp", lambda e, go=go, n=n, c0=c0: e.dma_start(out=mixT[n * 128:(n + 1) * 128, c0:c0 + 512], in_=go[:]),
                        sgo, reads=[b_go], writes=[b_mixT])
        P.phase_end()

    if upto >= 2:
        l0b()

    def l0c():
        P.phase_begin()
        conv_batch(3)
        lw32, b_lw32 = P.sb("lw32", [128, 32, 128], F32)
        lw, b_lw = P.sb("lw", [128, 32, 128], BF16)
        lb, b_lb = P.sb("lb", [128, 32], F32)
        lam, b_lam = P.sb("lam", [128, 16], F32)
        cl, b_cl = P.sb("cl", [128, 16], F32)
        cw, b_cw = P.sb("cw", [128, 8, 6], F32)
        dma("sp", lambda e: e.dma_start(out=lw32[:], in_=lruw_in), P.sem("misc"), reads=[b_in], writes=[b_lw32])
        dma("sp", lambda e: e.dma_start(out=lb[:], in_=lrub_in), P.sem("misc"), reads=[b_in], writes=[b_lb])
        dma("sp", lambda e: e.dma_start(out=lam[:], in_=lam_in), P.sem("misc"), reads=[b_in], writes=[b_lam])
        dma("sp", lambda e: e.dma_start(out=cw[:], in_=conv_in), P.sem("misc"), reads=[b_in], writes=[b_cw])
        op("dve", lambda e: e.tensor_copy(out=lw[:], in_=lw32[:]), reads=[b_lw32], writes=[b_lw])
        op("act", lambda e: e.activation(out=cl[:], in_=lam[:], func=AF.Exp, scale=-1.0), reads=[b_lam], writes=[b_cl])
        op("act", lambda e: e.activation(out=cl[:], in_=cl[:], func=AF.Ln, bias=cst[:, 1:2]), reads=[b_cl], writes=[b_cl])
        op("dve", lambda e: e.tensor_scalar(out=cl[:], in0=cl[:], scalar1=-8.0, scalar2=None, op0=ALU.mult),
           reads=[b_cl], writes=[b_cl])
        xr_r = Rot([P.sb(f"xr{i}", [128, SEQ], F32) + (P.sem(f"c_xr{i}"),) for i in range(1)])
        xc, b_xc = P.sb("xc", [128, SEQ], F32)
        xcb, b_xcb = P.sb("xcb", [128, SEQ], BF16)
        rr, b_rr = P.sb("rr", [128, SEQ], F32)
        ii, b_ii = P.sb("ii", [128, SEQ], F32)
        a2, b_a2 = P.sb("a2", [128, SEQ], F32)
        hh = [P.sb(f"hh{d}", [128, NOUT if d == 0 else SEQ], F32) for d in range(2)]
        gy_r = Rot([P.sb(f"gy{i}", [128, NOUT], F32) + (P.sem(f"c_gy{i}"),) for i in range(1)])
        bo_r = Rot([P.sb(f"bo{i}", [128, NOUT], BF16) + (P.sem(f"c_bo{i}"),) for i in range(1)])
        psr = Rot(list(zip(ps, b_ps)))
        segs = [(0, NCTX), (NCTX, SEQ)]
        for g in range(8):
            xr, b_xr, sxr = xr_r.next()
            dma("sp", lambda e, xr=xr, g=g: e.dma_start(out=xr[:], in_=xrT[g * 128:(g + 1) * 128, :]),
                sxr, reads=[b_xrT], writes=[b_xr])
            gy, b_gy, sgy = gy_r.next()
            dma("sp", lambda e, gy=gy, g=g: e.dma_start(out=gy[:], in_=gyT[g * 128:(g + 1) * 128, :]),
                sgy, reads=[b_gyT], writes=[b_gy])
            op("act", lambda e, xr=xr, g=g: e.activation(
                out=xc[:], in_=xr[:], func=AF.Identity, scale=cw[:, g, 2:3], bias=cw[:, g, 5:6]),
               reads=[b_xr, b_cw], writes=[b_xc])
            for (a, b) in segs:
                for k, o in enumerate([-2, -1, 1, 2]):
                    tap = o + 2
                    lo, hi = max(a, a - o), min(b, b - o)
                    op("dve", lambda e, xr=xr, g=g, tap=tap, lo=lo, hi=hi, o=o: e.scalar_tensor_tensor(
                        out=xc[:, lo:hi], in0=xr[:, lo + o:hi + o], scalar=cw[:, g, tap:tap + 1], in1=xc[:, lo:hi],
                        op0=ALU.mult, op1=ALU.add), reads=[b_xr, b_cw, b_xc], writes=[b_xc])
            op("act", lambda e: e.activation(out=xcb[:], in_=xc[:], func=AF.Copy), reads=[b_xc], writes=[b_xcb])
            for d in range(2):
                N = NOUT if d == 0 else SEQ
                h_t, b_h = hh[d]
                groups = [(c0, min(512, N - c0)) for c0 in range(0, N, 512)]
                for gate, (dst, b_dst) in enumerate([(rr, b_rr), (ii, b_ii)]):
                    wi = d * 16 + gate * 8 + g
                    for (c0, pc) in groups:
                        pst, b_pst = psr.next()
                        op("pe", lambda e, pst=pst, wi=wi, c0=c0, pc=pc: e.matmul(
                            out=pst[:, 0:pc], lhsT=lw[:, wi, :], rhs=xcb[:, c0:c0 + pc], start=True, stop=True),
                           reads=[b_lw, b_xcb], writes=[b_pst])
                        op("act", lambda e, pst=pst, dst=dst, wi=wi, c0=c0, pc=pc: e.activation(
                            out=dst[:, c0:c0 + pc], in_=pst[:, 0:pc], func=AF.Sigmoid, bias=lb[:, wi:wi + 1]),
                           reads=[b_pst, b_lb], writes=[b_dst])
                op("act", lambda e, d=d, g=g, N=N: e.activation(
                    out=rr[:, 0:N], in_=rr[:, 0:N], func=AF.Exp, scale=cl[:, d * 8 + g:d * 8 + g + 1]),
                   reads=[b_rr, b_cl], writes=[b_rr])
                op("act", lambda e, N=N: e.activation(out=a2[:, 0:N], in_=rr[:, 0:N], func=AF.Square),
                   reads=[b_rr], writes=[b_a2])
                op("act", lambda e, N=N: e.activation(out=a2[:, 0:N], in_=a2[:, 0:N], func=AF.Sqrt, scale=-1.0, bias=cst[:, 1:2]),
                   reads=[b_a2], writes=[b_a2])
                op("dve", lambda e, N=N: e.tensor_tensor(out=ii[:, 0:N], in0=ii[:, 0:N], in1=xc[:, 0:N], op=ALU.mult),
                   reads=[b_ii, b_xc], writes=[b_ii])
                op("dve", lambda e, N=N: e.tensor_tensor(out=ii[:, 0:N], in0=ii[:, 0:N], in1=a2[:, 0:N], op=ALU.mult),
                   reads=[b_ii, b_a2], writes=[b_ii])
                if d == 0:
                    op("dve", lambda e, h_t=h_t: e.tensor_tensor_scan(
                        out=h_t[:, 0:NCTX], data0=rr[:, 0:NCTX], data1=ii[:, 0:NCTX], initial=0.0,
                        op0=ALU.mult, op1=ALU.add), reads=[b_rr, b_ii], writes=[b_h])
                    op("dve", lambda e, h_t=h_t: e.tensor_tensor_scan(
                        out=h_t[:, NCTX:NOUT], data0=rr[:, NCTX:NOUT], data1=ii[:, NCTX:NOUT],
                        initial=h_t[:, NCTX - 1:NCTX], op0=ALU.mult, op1=ALU.add),
                       reads=[b_rr, b_ii, b_h], writes=[b_h])
                else:
                    op("dve", lambda e, h_t=h_t: e.tensor_tensor_scan(
                        out=h_t[:, 0:NCTX][:, ::-1], data0=rr[:, 0:NCTX][:, ::-1], data1=ii[:, 0:NCTX][:, ::-1],
                        initial=0.0, op0=ALU.mult, op1=ALU.add), reads=[b_rr, b_ii], writes=[b_h])
                    op("dve", lambda e, h_t=h_t: e.tensor_tensor_scan(
                        out=h_t[:, NCTX:SEQ][:, ::-1], data0=rr[:, NCTX:SEQ][:, ::-1], data1=ii[:, NCTX:SEQ][:, ::-1],
                        initial=h_t[:, 0:1], op0=ALU.mult, op1=ALU.add),
                       reads=[b_rr, b_ii, b_h], writes=[b_h])
            bo, b_bo, sbo = bo_r.next()
            h0, b_h0 = hh[0]
            h1, b_h1 = hh[1]
            op("dve", lambda e, h0=h0, h1=h1: e.tensor_tensor(out=h0[:, 0:NOUT], in0=h0[:, 0:NOUT], in1=h1[:, 0:NOUT], op=ALU.add),
               reads=[b_h0, b_h1], writes=[b_h0])
            op("dve", lambda e, bo=bo, h0=h0, gy=gy: e.tensor_tensor(out=bo[:], in0=h0[:, 0:NOUT], in1=gy[:], op=ALU.mult),
               reads=[b_h0, b_gy], writes=[b_bo])
            dma("sp", lambda e, bo=bo, g=g: e.dma_start(out=mixT[1024 + g * 128:1024 + (g + 1) * 128, :], in_=bo[:]),
                sbo, reads=[b_bo], writes=[b_mixT])
        P.phase_end()

    if upto >= 3:
        l0c()
        mod_parts(0, [2, 3, 4, 5])
    if upto >= 4:
        proj_resid_phase("l0d", 0, "w_out", mixT, b_mixT, lambda s0: s0, xT_in, b_in, xA, b_xA, L0_TILES)
    if upto >= 5:
        mlp_phase(0, xA, b_xA, xB, b_xB, L0_TILES)
        mod_parts(1, [0, 1, 2, 3, 4, 5])

    def l1a():
        P.phase_begin()
        TM = 512
        hTs = [P.sb(f"hT{i}", [128, KC, TM], BF16) for i in range(2)]
        rstds = [P.sb(f"rstd{i}", [128, TM], F32) for i in range(2)]
        xch = Rot([P.sb(f"xch{i}", [128, TM], F32) + (P.sem(f"d_x{i}"),) for i in range(4)])
        sq = Rot(P.sbn("sq", [128, TM], BF16, 2))
        wts = WPool([P.sb(f"w{i}", [128, KC, 128], BF16) + (P.sem(f"d_w{i}"),) for i in range(4)])
        wtk = Rot([P.sb(f"wk{i}", [128, 4, KC, 128], BF16) + (P.sem(f"d_wk{i}"),) for i in range(2)])
        obh = Rot([P.sb(f"obh{i}", [128, TM], BF16) + (P.sem(f"d_obh{i}"),) for i in range(4)])
        vob = Rot([P.sb(f"vob{i}", [128, 512], BF16) + (P.sem(f"d_vob{i}"),) for i in range(3)])
        psr = Rot(list(zip(ps, b_ps)))
        def do_norm(s0, T, slot):
            hT, b_hT = hTs[slot]
            rstd, b_rstd = rstds[slot]
            norm_tile(xB, b_xB, s0, T, mod_ranges(1, 0, s0, T), hT, b_hT, xch, sq, rstd, b_rstd,
                      [psr.next(), psr.next()])

        def do_tile(s0, T, slot, nxt_tile):
            own = (s0, T) in L1_TILES
            hT, b_hT = hTs[slot]

            def cp_epi(dst, b_dst, base, d0, scale):
                def epi(g, pls):
                    n = g[0] - base
                    o_t, b_o, so = obh.next()
                    for (pst, b_pst, c0, pc) in pls[0]:
                        op("act", lambda e, pst=pst, c0=c0, pc=pc: e.activation(
                            out=o_t[:, c0:c0 + pc], in_=pst[:, 0:pc], func=AF.Copy, scale=scale),
                           reads=[b_pst], writes=[b_o])
                    dma("sp", lambda e: e.dma_start(out=dst[n * 128:(n + 1) * 128, d0:d0 + T], in_=o_t[:, 0:T]),
                        so, reads=[b_o], writes=[b_dst])
                return epi

            if own:
                lin("w_qkv", KC, [[n] for n in range(16)], hT, b_hT, T,
                    cp_epi(q1T, b_q1T, 0, s0 - NCTX, 128.0 ** -0.5), wts, psr,
                    nxt=[("w_qkv", 16 + n) for n in range(3)])
            lin("w_qkv", KC, [[16 + n] for n in range(16)], hT, b_hT, T, cp_epi(k1T, b_k1T, 16, s0, 1.0), wts, psr,
                nxt=nxt_tile)

            def v_epi(gi, tb, pst, b_pst):
                o_t, b_o, so = vob.next()
                op("act", lambda e: e.activation(out=o_t[:], in_=pst[:], func=AF.Copy), reads=[b_pst], writes=[b_o])
                dma("sp", lambda e: e.dma_start(
                    out=v1S[s0 + tb * 128:s0 + (tb + 1) * 128, gi * 512:(gi + 1) * 512], in_=o_t[:]),
                    so, reads=[b_o], writes=[b_v1S])

            lin_tok("w_qkv", 32, 4, hT, b_hT, T, v_epi, wtk, psr)

        tl = L1_TILES + L1_KV_TILES
        do_norm(tl[0][0], tl[0][1], 0)
        for ti, (s0_, T_) in enumerate(tl):
            if ti + 1 < len(tl):
                do_norm(tl[ti + 1][0], tl[ti + 1][1], (ti + 1) % 2)
            if ti + 1 == len(tl):
                nt = []
            elif tl[ti + 1] in L1_TILES:
                nt = [("w_qkv", n) for n in range(3)]
            else:
                nt = [("w_qkv", 16 + n) for n in range(3)]
            do_tile(s0_, T_, ti % 2, nt)
        P.phase_end()

    def l1b():
        P.phase_begin()
        lists = na_lists()
        NKB = NOUT // 128
        q_r = Rot([P.sb(f"nq{i}", [128, NOWN], BF16) + (P.sem(f"e_q{i}"),) for i in range(2)])
        k_r = Rot([P.sb(f"nk{i}", [128, NOUT], BF16) + (P.sem(f"e_k{i}"),) for i in range(2)])
        v_r = Rot([P.sb(f"nv{i}", [128, NKB, 128], BF16) + (P.sem(f"e_v{i}"),) for i in range(2)])
        bf_r = Rot([P.sb(f"nbf{i}", [128, 13, 128], F32) + (P.sem(f"e_b{i}"),) for i in range(2)])
        bb_r = Rot(P.sbn("nbb", [128, 13, 128], BF16, 2))
        o_r = Rot([P.sb(f"no{i}", [128, NOWN], BF16) + (P.sem(f"e_o{i}"),) for i in range(2)])
        pT_r = Rot(P.sbn("npT", [128, 7, 128], BF16, 3))
        rd_r = Rot(P.sbn("nrd", [128, 128], F32, 3))
        sbanks = [((ps[0], b_ps[0]), (ps[1], b_ps[1])), ((ps[2], b_ps[2]), (ps[3], b_ps[3])),
                  ((ps[4], b_ps[4]), (ps[5], b_ps[5]))]
        obanks = [(ps[6], Buf("po6")), (ps[7], Buf("po7"))]
        cnt = [0]
        prev = [None]

        def att_front(i, q_t, b_q, k_t, b_k, bb, b_bb):
            kl = lists[i]
            (pa, b_pa), (pb, b_pb) = sbanks[cnt[0] % 3]
            po, b_po = obanks[cnt[0] % 2]
            cnt[0] += 1
            pT, b_pT = pT_r.next()
            for idx, (kb, bid) in enumerate(kl):
                pst, b_pst = (pa, b_pa) if idx < 4 else (pb, b_pb)
                col = (idx % 4) * 128
                op("pe", lambda e, pst=pst, col=col, kb=kb, bid=bid: e.matmul(
                    out=pst[:, col:col + 128], lhsT=k_t[:, kb * 128:(kb + 1) * 128], rhs=q_t[:, i * 128:(i + 1) * 128],
                    start=True, stop=(bid is None)), reads=[b_k, b_q], writes=[b_pst])
                if bid is not None:
                    op("pe", lambda e, pst=pst, col=col, bid=bid: e.matmul(
                        out=pst[:, col:col + 128], lhsT=ident_b[:], rhs=bb[:, bid, :], start=False, stop=True),
                       reads=[b_identb, b_bb], writes=[b_pst])
            nk = len(kl)
            op("act", lambda e: e.activation(
                out=pT[:, 0:4, :], in_=pa[:, :].rearrange("p (g j) -> p g j", j=128), func=AF.Exp),
               reads=[b_pa], writes=[b_pT])
            op("act", lambda e: e.activation(
                out=pT[:, 4:nk, :], in_=pb[:, 0:(nk - 4) * 128].rearrange("p (g j) -> p g j", j=128), func=AF.Exp),
               reads=[b_pb], writes=[b_pT])
            return dict(kl=kl, nk=nk, pT=pT, b_pT=b_pT, po=po, b_po=b_po)

        def att_back(c):
            kl, nk, pT, b_pT, po, b_po = c["kl"], c["nk"], c["pT"], c["b_pT"], c["po"], c["b_po"]
            i, h, v_t, b_v, o_t, b_o, so = c["i"], c["h"], c["v_t"], c["b_v"], c["o_t"], c["b_o"], c["so"]
            for idx, (kb, bid) in enumerate(kl):
                op("pe", lambda e, kb=kb, idx=idx: e.matmul(
                    out=po[:, 0:128], lhsT=v_t[:, kb, :], rhs=pT[:, idx, :], start=(idx == 0), stop=(idx == nk - 1)),
                   reads=[b_v, b_pT], writes=[b_po])
            for idx, (kb, bid) in enumerate(kl):
                op("pe", lambda e, idx=idx: e.matmul(
                    out=po[:, 128:256], lhsT=ones1_bf[:], rhs=pT[:, idx, :], start=(idx == 0), stop=(idx == nk - 1)),
                   reads=[b_ones1, b_pT], writes=[b_po])
            rd, b_rd = rd_r.next()
            op("dve", lambda e: e.reciprocal(out=rd[:], in_=po[:, 128:256]), reads=[b_po], writes=[b_rd])
            op("dve", lambda e: e.tensor_tensor(
                out=o_t[:, i * 128:(i + 1) * 128], in0=po[:, 0:128], in1=rd[:], op=ALU.mult),
               reads=[b_po, b_rd], writes=[b_o])
            if i == 15:
                dma("sp", lambda e: e.dma_start(out=at1T[h * 128:(h + 1) * 128, :], in_=o_t[:]),
                    so, reads=[b_o], writes=[b_at1T])

        for h in range(16):
            q_t, b_q, sq_ = q_r.next()
            dma("sp", lambda e, q_t=q_t, h=h: e.dma_start(out=q_t[:], in_=q1T[h * 128:(h + 1) * 128, :]),
                sq_, reads=[b_q1T], writes=[b_q])
            k_t, b_k, sk = k_r.next()
            dma("sp", lambda e, k_t=k_t, h=h: e.dma_start(out=k_t[:], in_=k1T[h * 128:(h + 1) * 128, :]),
                sk, reads=[b_k1T], writes=[b_k])
            v_t, b_v, sv = v_r.next()
            dma("sp", lambda e, v_t=v_t, h=h: e.dma_start(
                out=v_t[:], in_=v1S[:, h * 128:(h + 1) * 128].rearrange("(b p) v -> p b v", p=128)),
                sv, reads=[b_v1S], writes=[b_v])
            bf, b_bf, sbf = bf_r.next()
            dma("sp", lambda e, bf=bf, h=h: e.dma_start(out=bf[:], in_=nab_in[h]), sbf, reads=[b_in], writes=[b_bf])
            bb, b_bb = bb_r.next()
            op("pool", lambda e, bb=bb, bf=bf: e.tensor_copy(out=bb[:], in_=bf[:]), reads=[b_bf], writes=[b_bb])
            o_t, b_o, so = o_r.next()
            for i in range(16):
                c = att_front(i, q_t, b_q, k_t, b_k, bb, b_bb)
                c.update(i=i, h=h, v_t=v_t, b_v=b_v, o_t=o_t, b_o=b_o, so=so)
                if prev[0] is not None:
                    att_back(prev[0])
                prev[0] = c
        att_back(prev[0])
        P.phase_end()

    def l1e():
        P.phase_begin()
        T = 512
        xch = Rot([P.sb(f"xch{i}", [128, T], F32) + (P.sem(f"f_x{i}"),) for i in range(4)])
        sq = Rot(P.sbn("sq", [128, T], BF16, 2))
        rstd, b_rstd = P.sb("rstd", [128, T], F32)
        xn_r = Rot(P.sbn("xn", [128, T], F32, 3))
        ot = [P.sb(f"ot{i}", [128, D], F32) + (P.sem(f"f_o{i}"),) for i in range(4)]
        psr = Rot(list(zip(ps, b_ps)))
        ident_f = umat[:, 4, :]
        def do_tile(ti):
            s0 = NCTX + ti * T
            sumsq_rstd(xB, b_xB, s0, T, xch, sq, rstd, b_rstd, [psr.next()], ones_bf, b_ones, KC)
            for cg in range(4):
                pbanks = [psr.next() for _ in range(4)]
                for cc in range(4):
                    c = cg * 4 + cc
                    x_t, b_x, sx = xch.next()
                    dma("sp", lambda e, x_t=x_t, c=c: e.dma_start(out=x_t[:], in_=xB[c * 128:(c + 1) * 128, s0:s0 + T]),
                        sx, reads=[b_xB], writes=[b_x])
                    xn, b_xn = xn_r.next()
                    op("dve", lambda e, xn=xn, x_t=x_t, c=c: e.scalar_tensor_tensor(
                        out=xn[:], in0=x_t[:], scalar=ngt[:, 4, c:c + 1], in1=rstd[:], op0=ALU.mult, op1=ALU.mult),
                       reads=[b_x, b_ngt, b_rstd], writes=[b_xn])
                    for tb in range(4):
                        pst, b_pst = pbanks[tb]
                        op("pe", lambda e, pst=pst, xn=xn, tb=tb, cc=cc: e.matmul(
                            out=pst[:, cc * 128:(cc + 1) * 128], lhsT=xn[:, tb * 128:(tb + 1) * 128], rhs=ident_f,
                            start=True, stop=True), reads=[b_xn, b_umat], writes=[b_pst])
                for tb in range(4):
                    pst, b_pst = pbanks[tb]
                    o_t, b_o, so = ot[tb]
                    eng = "act" if tb % 2 == 0 else "dve"
                    if eng == "act":
                        op("act", lambda e, o_t=o_t, pst=pst, cg=cg: e.activation(
                            out=o_t[:, cg * 512:(cg + 1) * 512], in_=pst[:], func=AF.Copy), reads=[b_pst], writes=[b_o])
                    else:
                        op("dve", lambda e, o_t=o_t, pst=pst, cg=cg: e.tensor_copy(
                            out=o_t[:, cg * 512:(cg + 1) * 512], in_=pst[:]), reads=[b_pst], writes=[b_o])
            for tb in range(4):
                o_t, b_o, so = ot[tb]
                r0 = ti * T + tb * 128
                dma("sp", lambda e, o_t=o_t, r0=r0: e.dma_start(out=out_d[r0:r0 + 128, :], in_=o_t[:]),
                    so, reads=[b_o], writes=[b_out])

        for ti_ in range(NOWN // T):
            do_tile(ti_)
        P.phase_end()

    if upto >= 6:
        l1a()
    if upto >= 7:
        l1b()
    if upto >= 8:
        proj_resid_phase("l1c", 1, "w_no", at1T, b_at1T, lambda s0: s0 - NCTX, xB, b_xB, xA, b_xA, L1_TILES)
    if upto >= 9:
        mlp_phase(1, xA, b_xA, xB, b_xB, L1_TILES)
    if upto >= 10:
        l1e()
    P.phase_begin()
    P.phase_end()
    P.top.close()
    return P


def _cm(W):
    K, N = W.shape
    return np.ascontiguousarray(W.reshape(K // 128, 128, N // 128, 128).transpose(2, 1, 0, 3))


def _col(v):
    return np.ascontiguousarray(v.reshape(-1, 128).T)


def _rope_tables(flip):
    n_freq = 32
    inv_freq = (np.float32(10000.0) ** (-np.arange(n_freq, dtype=np.float32) / np.float32(n_freq))).astype(np.float32)
    t = np.arange(NLAT)
    if flip:
        t = NLAT - 1 - t
    row = (t // GRID_W).astype(np.float32)
    col = (t % GRID_W).astype(np.float32)
    p = np.arange(128)
    pos = np.where((p // 64)[:, None] == 0, row[None, :], col[None, :]).astype(np.float32)
    ang = (pos * inv_freq[p % 32][:, None]).astype(np.float32)
    sign = np.where((p % 64) < 32, -1.0, 1.0).astype(np.float32)[:, None]
    cos = np.cos(ang).astype(np.float32)
    sin = (np.sin(ang).astype(np.float32) * sign).astype(np.float32)
    tab = np.zeros((4, 128, SEQ), np.float32)
    sc = np.float32(128.0 ** -0.5)
    tab[0, :, :NCTX] = sc
    tab[2, :, :NCTX] = 1.0
    tab[0, :, NCTX:] = cos * sc
    tab[1, :, NCTX:] = sin * sc
    tab[2, :, NCTX:] = cos
    tab[3, :, NCTX:] = sin
    return tab


def _umat():
    s = np.arange(128)[:, None]
    t = np.arange(128)[None, :]
    same = (s // 64) == (t // 64)
    u = np.zeros((128, 6, 128), np.float32)
    u[:, 0, :] = np.where(same & (s <= t), -1.0 / 16.0, 0.0)
    u[:, 1, :] = np.where(same & (s >= t), -1.0 / 16.0, 0.0)
    u[:, 2, :] = np.where(same & (s <= t), 1.0, 0.0)
    u[:, 3, :] = np.where(same & (s >= t), 1.0, 0.0)
    u[:, 4, :] = np.eye(128, dtype=np.float32)
    u[:, 5, :] = 1.0
    return u


def _na_bias(rel_bias, flip):
    lists = na_lists()
    out = np.full((16, 128, 13, 128), NEG, np.float32)
    done = {}
    for i in range(16):
        for (kb, bid) in lists[i]:
            if bid is None:
                continue
            j = kb - 2
            ql = i * 128 + np.arange(128)
            kl = j * 128 + np.arange(128)
            qg = (NLAT - 1 - ql) if flip else ql
            kg = (NLAT - 1 - kl) if flip else kl
            qr, qc = qg // GRID_W, qg % GRID_W
            kr, kc = kg // GRID_W, kg % GRID_W
            rs = np.clip(qr - 4, 0, 64 - 8)
            cs = np.clip(qc - 8, 0, GRID_W - 16)
            valid = ((kr[:, None] >= rs[None, :]) & (kr[:, None] < rs[None, :] + 8) &
                     (kc[:, None] >= cs[None, :]) & (kc[:, None] < cs[None, :] + 16))
            dr = np.clip(kr[:, None] - qr[None, :] + 7, 0, 14)
            dc = np.clip(kc[:, None] - qc[None, :] + 15, 0, 30)
            blk = np.where(valid[None], rel_bias[:, dr, dc], np.float32(NEG)).astype(np.float32)
            if bid in done:
                assert np.array_equal(done[bid], blk), (i, j, bid)
            else:
                done[bid] = blk
                out[:, :, bid, :] = blk
    return out


_PROG = {}


def _get_prog():
    if "p" not in _PROG:
        _PROG["p"] = build()
    return _PROG["p"]


def make_in_maps(x, c, ctx, c_ctx, mod_w, mod_b, norm1_g, norm2_g, mlp_w_up, mlp_w_down,
                 ev_w_in, ev_w_out, gla_gate_up, gla_gate_b, gla_norm_g, lru_conv_w, lru_conv_b,
                 lru_w_a, lru_b_a, lru_w_x, lru_b_x, lru_lambda, na_w_qkv, na_w_out, na_rel_bias,
                 final_norm_g):
    f = lambda a: np.asarray(a, dtype=np.float32)
    x, c, ctx, c_ctx = f(x), f(c), f(ctx), f(c_ctx)
    shared = {}
    for l in range(2):
        shared[f"modw{l}"] = _cm(f(mod_w[l]))
        mb = _col(f(mod_b[l]))
        shared[f"modb{l}"] = np.ascontiguousarray(np.stack([mb, mb], axis=-1))
        shared[f"w_up{l}"] = _cm(f(mlp_w_up[l]))
        shared[f"w_dn{l}"] = _cm(f(mlp_w_down[l]))
    shared["normg"] = np.ascontiguousarray(np.stack(
        [_col(f(norm1_g[0])), _col(f(norm2_g[0])), _col(f(norm1_g[1])), _col(f(norm2_g[1])), _col(f(final_norm_g))], axis=1))
    shared["w_out"] = _cm(f(ev_w_out[0]))
    shared["w_qkv"] = _cm(f(na_w_qkv[0]))
    shared["w_no"] = _cm(f(na_w_out[0]))
    shared["umat"] = _umat()
    shared["gla_g"] = _col(f(gla_norm_g[0]))
    W = f(ev_w_in[0])
    p = np.arange(128)
    partner = np.where((p % 64) < 32, p + 32, p - 32)
    perm512 = (np.arange(4)[:, None] * 128 + partner[None, :]).reshape(-1)
    wq, wk = W[:, 0:512], W[:, 512:1024]
    wv, wog = W[:, 1024:2048], W[:, 2048:3072]
    wgl, wxr, wyr = W[:, 3072:3104], W[:, 3104:4128], W[:, 4128:5152]
    per_half = []
    for half in range(2):
        flip = half == 1
        dirs = [1, 0] if flip else [0, 1]
        gl = np.zeros((D, 128), np.float32)
        gl[:, 0:16] = wgl[:, dirs[0] * 16:dirs[0] * 16 + 16]
        gl[:, 16:32] = wgl[:, dirs[1] * 16:dirs[1] * 16 + 16]
        wext = np.concatenate([wq, wq[:, perm512], wk, wk[:, perm512], wv, wog, wxr, wyr, gl], axis=1)
        hd = {"w_in": _cm(wext), "rope": _rope_tables(flip)}
        gu = np.zeros((17, 2, 512), np.float32)
        for di, dr in enumerate(dirs):
            gu[0:16, di, :] = f(gla_gate_up[0, dr])
            gu[16, di, :] = f(gla_gate_b[0, dr])
        hd["gate_up"] = gu
        lw = np.zeros((128, 32, 128), np.float32)
        lb = np.zeros((128, 32), np.float32)
        lam = np.zeros((128, 16), np.float32)
        for di, dr in enumerate(dirs):
            for g in range(8):
                lw[:, di * 16 + g, :] = f(lru_w_a[0, dr, g])
                lw[:, di * 16 + 8 + g, :] = f(lru_w_x[0, dr, g])
                lb[:, di * 16 + g] = f(lru_b_a[0, dr, g * 128:(g + 1) * 128])
                lb[:, di * 16 + 8 + g] = f(lru_b_x[0, dr, g * 128:(g + 1) * 128])
                lam[:, di * 8 + g] = f(lru_lambda[0, dr, g * 128:(g + 1) * 128])
        hd["lru_w"], hd["lru_b"], hd["lru_lam"] = lw, lb, lam
        cw = np.zeros((128, 8, 6), np.float32)
        cwt = f(lru_conv_w[0])
        for g in range(8):
            sl = slice(g * 128, (g + 1) * 128)
            if not flip:
                for j in range(4):
                    cw[:, g, j] = cwt[j, sl]
            else:
                for j in range(4):
                    cw[:, g, 4 - j] = cwt[j, sl]
            cw[:, g, 5] = f(lru_conv_b[0])[sl]
        hd["conv"] = cw
        hd["na_bias"] = _na_bias(f(na_rel_bias[0]), flip)
        per_half.append(hd)
    in_maps = []
    for core in range(8):
        b, half = core // 2, core % 2
        xl = x[b][::-1] if half else x[b]
        cl = ctx[b][::-1] if half else ctx[b]
        m = dict(shared)
        m.update(per_half[half])
        m["xT"] = np.ascontiguousarray(np.concatenate([cl, xl], axis=0).T)
        m["cT"] = np.ascontiguousarray(np.stack([_col(c[b]), _col(c_ctx)], axis=-1))
        in_maps.append(m)
    return in_maps


def kernel(**inputs):
    P = _get_prog()
    in_maps = make_in_maps(**inputs)
    res = run_bass_kernel_spmd(P.nc, in_maps, core_ids=list(range(8)))
    out = np.zeros((4, NLAT, D), np.float32)
    for core in range(8):
        b, half = core // 2, core % 2
        o = np.asarray(res.results[core]["out"])
        if half:
            out[b, NLAT - NOWN:] = o[::-1]
        else:
            out[b, :NOWN] = o
    return out
```

```python
import contextlib
import numpy as np
import concourse.bass as bass
import concourse.mybir as mybir
from concourse.bass_utils import run_bass_kernel_spmd

F32 = mybir.dt.float32
BF16 = mybir.dt.bfloat16
AF = mybir.ActivationFunctionType
ALU = mybir.AluOpType

D = 2048
KC = 16
DFF = 8192
NCTX = 256
NLAT = 4096
SEQ = NCTX + NLAT
NFULL = 2304
NOUT = NCTX + NFULL
NOWN = 2048
EPS = 1e-6
GRID_W = 64


class Buf:
    __slots__ = ("name", "w", "r", "multi")

    def __init__(self, name, multi=False):
        self.name = name
        self.w = {}
        self.r = {}
        self.multi = multi


class Sem:
    __slots__ = ("name", "h", "count", "nobarrier")

    def __init__(self, name):
        self.name = name
        self.h = None
        self.count = 0
        self.nobarrier = False


ENGS = ("pe", "act", "dve", "pool", "sp")
ENAME = {"pe": "tensor", "act": "scalar", "dve": "vector", "pool": "gpsimd", "sp": "sync"}


class Sched:
    def __init__(self, nc, stack):
        self.nc = nc
        self.streams = {e: [] for e in ENGS}
        self.esem = {}
        self.sems = []
        self.seen = {e: {} for e in ENGS}
        self.stack = stack
        for e in ENGS:
            self.esem[e] = self.new_sem("e_" + e)
        self.n = 0

    def new_sem(self, name):
        s = Sem(name)
        s.h = self.stack.enter_context(self.nc.semaphore(name))
        self.sems.append(s)
        return s

    def _deps(self, eng, reads, writes):
        waits = {}

        def add(d, skip_same):
            for s, (v, e) in d.items():
                if e == eng and (eng == "pe" or skip_same):
                    continue
                if waits.get(s, 0) < v:
                    waits[s] = v

        for b in reads:
            add(b.w, False)
        for b in writes:
            if not b.multi:
                add(b.w, False)
            add(b.r, False)
        out = []
        seen = self.seen[eng]
        for s, v in waits.items():
            if seen.get(s, 0) >= v:
                continue
            seen[s] = v
            out.append((s, v))
        return out

    def _commit(self, reads, writes, s, v, e):
        for b in reads:
            b.r[s] = (v, e)
        for b in writes:
            if b.multi:
                b.w[s] = (v, e)
            else:
                b.w = {s: (v, e)}
                b.r = {}

    def op(self, eng, fn, reads=(), writes=()):
        waits = self._deps(eng, reads, writes)
        s = self.esem[eng]
        s.count += 1
        self.streams[eng].append((waits, fn, s, 1))
        self._commit(reads, writes, s, s.count, eng)
        self.n += 1

    def dma(self, q, fn, sem, reads=(), writes=()):
        waits = self._deps(q, reads, writes)
        sem.count += 16
        self.streams[q].append((waits, fn, sem, 16))
        self._commit(reads, writes, sem, sem.count, "dma")
        self.n += 1

    def barrier(self):
        for e in ENGS:
            waits = []
            seen = self.seen[e]
            for s in self.sems:
                if s.nobarrier:
                    continue
                if s.count > 0 and seen.get(s, 0) < s.count:
                    seen[s] = s.count
                    waits.append((s, s.count))
            self.streams[e].append((waits, None, None, 0))

    def emit(self):
        nc = self.nc
        with nc.Block() as block:
            def make(engname):
                stream = self.streams[engname]

                def body(eng):
                    for (waits, fn, sem, inc) in stream:
                        for (s, v) in waits:
                            eng.wait_ge(s.h, v)
                        if fn is not None:
                            fn(eng).then_inc(sem.h, inc)
                return body

            for e in ENGS:
                getattr(block, ENAME[e])(make(e))
        self.streams = {e: [] for e in ENGS}


def pieces_of(T):
    n = (T + 511) // 512
    assert T % n == 0
    pc = T // n
    return [(i * pc, pc) for i in range(n)]


class Prog:
    def __init__(self, debug=()):
        self.debug = set(debug)
        self.nc = bass.Bass("TRN2", target_bir_lowering=False)
        self.top = contextlib.ExitStack()
        self.S = Sched(self.nc, self.top)
        self.ph = None
        self.inputs = {}
        self.free_sems = []
        self.ph_sems = []
        self.phase_no = 0

    def din(self, name, shape, dt=F32):
        t = self.nc.dram_tensor(name, list(shape), dt, kind="ExternalInput").ap()
        self.inputs[name] = (tuple(shape), dt)
        return t

    def dscr(self, name, shape, dt):
        kind = "ExternalOutput" if name in self.debug else "Internal"
        t = self.nc.dram_tensor(name, list(shape), dt, kind=kind).ap()
        return t, Buf(name, multi=True)

    def sb(self, name, shape, dt, top=False):
        st = self.top if top else self.ph
        name = f"s{self.phase_no}_{name}"
        t = st.enter_context(self.nc.sbuf_tensor(name, list(shape), dt))
        return t, Buf(name)

    def sbn(self, name, shape, dt, n):
        return [self.sb(f"{name}{i}", shape, dt) for i in range(n)]

    def sem(self, name, persistent=False):
        if persistent:
            return self.S.new_sem(name)
        if self.free_sems:
            s = self.free_sems.pop()
        else:
            s = self.S.new_sem(f"pool{len(self.S.sems)}")
        self.ph_sems.append(s)
        return s

    def phase_begin(self):
        self.phase_no += 1
        self.ph = contextlib.ExitStack()
        self.ph.__enter__()

    def phase_end(self):
        self.S.barrier()
        self.S.emit()
        self.ph.__exit__(None, None, None)
        self.ph = None
        self.free_sems.extend(self.ph_sems)
        self.ph_sems = []


class Rot:
    def __init__(self, items):
        self.items = items
        self.i = 0

    def next(self):
        it = self.items[self.i % len(self.items)]
        self.i += 1
        return it


L0_TILES = [(0, 640), (640, 640), (1280, 640), (1920, 640)]
L0_OTHER = [(2560, 896), (3456, 896)]
L1_TILES = [(256, 512), (768, 512), (1280, 512), (1792, 512)]
L1_KV_TILES = [(0, 256), (2304, 256)]
NEG = -30000.0


def na_lists():
    res = []
    for i in range(16):
        l = [(0, None), (1, None)]
        if i <= 1:
            for j in range(4):
                l.append((2 + j, i * 4 + j))
        else:
            for j in range(i - 2, i + 3):
                l.append((2 + j, 8 + (j - i + 2)))
        res.append(l)
    return res


def build(debug=(), upto=99):
    P = Prog(debug)
    nc, S = P.nc, P.S
    op, dma = S.op, S.dma

    xT_in = P.din("xT", [D, SEQ])
    cT_in = P.din("cT", [128, KC, 2])
    modw_in = [P.din(f"modw{l}", [96, 128, KC, 128]) for l in range(2)]
    modb_in = [P.din(f"modb{l}", [128, 96, 2]) for l in range(2)]
    ng_in = P.din("normg", [128, 5, KC])
    win_in = P.din("w_in", [49, 128, KC, 128])
    wout_in = P.din("w_out", [16, 128, KC, 128])
    wup_in = [P.din(f"w_up{l}", [64, 128, KC, 128]) for l in range(2)]
    wdn_in = [P.din(f"w_dn{l}", [16, 128, 64, 128]) for l in range(2)]
    wqkv_in = P.din("w_qkv", [48, 128, KC, 128])
    wno_in = P.din("w_no", [16, 128, KC, 128])
    rope_in = P.din("rope", [4, 128, SEQ])
    gup_in = P.din("gate_up", [17, 2, 512])
    umat_in = P.din("umat", [128, 6, 128])
    glag_in = P.din("gla_g", [128, 8])
    lruw_in = P.din("lru_w", [128, 32, 128])
    lrub_in = P.din("lru_b", [128, 32])
    lam_in = P.din("lru_lam", [128, 16])
    conv_in = P.din("conv", [128, 8, 6])
    nab_in = P.din("na_bias", [16, 128, 13, 128])
    out_d = P.nc.dram_tensor("out", [NOWN, D], F32, kind="ExternalOutput").ap()
    b_out = Buf("out", multi=True)
    b_in = Buf("inputs")

    wbf = {}

    def wscr(name, shape):
        wbf[name] = P.dscr("wbf_" + name, shape, BF16)

    for l in range(2):
        for part in range(6):
            wscr(f"mod{l}_{part}", [16, 128, KC, 128])
    wscr("w_in", [49, 128, KC, 128])
    wscr("w_out", [16, 128, KC, 128])
    for l in range(2):
        wscr(f"w_up{l}", [64, 128, KC, 128])
        wscr(f"w_dn{l}", [16, 128, 64, 128])
    wscr("w_qkv", [48, 128, KC, 128])
    wscr("w_no", [16, 128, KC, 128])

    xA, b_xA = P.dscr("xA", [D, NOUT], F32)
    xB, b_xB = P.dscr("xB", [D, NOUT], F32)
    qT, b_qT = P.dscr("qT", [NOUT // 256, 128, 4, 256], BF16)
    kT, b_kT = P.dscr("kT", [SEQ // 256, 128, 4, 256], BF16)
    vS, b_vS = P.dscr("vS", [SEQ // 256, 128, 2, 1024], BF16)
    LS, b_LS = P.dscr("LS", [2, SEQ // 256, 128, 2, 512], F32)
    ogT, b_ogT = P.dscr("ogT", [1024, NOUT], F32)
    xrT, b_xrT = P.dscr("xrT", [1024, SEQ], F32)
    gyT, b_gyT = P.dscr("gyT", [1024, NOUT], F32)
    mixT, b_mixT = P.dscr("mixT", [D, NOUT], BF16)
    q1T, b_q1T = P.dscr("q1T", [D, NOWN], BF16)
    k1T, b_k1T = P.dscr("k1T", [D, NOUT], BF16)
    v1S, b_v1S = P.dscr("v1S", [NOUT, D], BF16)
    at1T, b_at1T = P.dscr("at1T", [D, NOWN], BF16)

    ones_bf, b_ones = P.sb("ones_bf", [128, 128], BF16, top=True)
    ones1_bf, b_ones1 = P.sb("ones1_bf", [128, 128], BF16, top=True)
    ones256_bf, b_ones256 = P.sb("ones256_bf", [128, 128], BF16, top=True)
    ident_b, b_identb = P.sb("ident_b", [128, 128], BF16, top=True)
    umat, b_umat = P.sb("umat", [128, 6, 128], F32, top=True)
    silc, b_silc = P.sb("silc", [128, KC, 2], BF16, top=True)
    modv = [P.sb(f"modv{l}", [128, 96, 2], F32, top=True) for l in range(2)]
    gsc = [P.sb(f"gsc{l}", [128, 2, KC, 2], F32, top=True) for l in range(2)]
    ngt, b_ngt = P.sb("ngt", [128, 5, KC], F32, top=True)
    cst, b_cst = P.sb("cst", [128, 2], F32, top=True)
    ps = [P.top.enter_context(nc.psum_tensor(f"ps{i}", [128, 512], F32)) for i in range(8)]
    b_ps = [Buf(f"ps{i}") for i in range(8)]


    P.phase_begin()

    def conv(name, src):
        t, b = wbf[name]
        sw = P.sem("wc_" + name, persistent=True)
        sw.nobarrier = True
        dma("pool", lambda e: e.dma_start(out=t, in_=src), sw, reads=[b_in], writes=[b])

    def conv_mod(l, parts):
        for part in parts:
            conv(f"mod{l}_{part}", modw_in[l][part * 16:(part + 1) * 16])

    def conv_batch(k):
        if k == 0:
            conv_mod(0, [0, 1])
            conv("w_in", win_in)
        elif k == 1:
            conv_mod(0, [2])
            conv("w_out", wout_in)
            conv_mod(0, [3, 4, 5])
            conv("w_up0", wup_in[0])
            conv("w_dn0", wdn_in[0])
        elif k == 2:
            conv_mod(1, [0, 1, 2, 3, 4, 5])
        elif k == 3:
            conv("w_qkv", wqkv_in)
            conv("w_no", wno_in)
        elif k == 4:
            conv("w_up1", wup_in[1])
        else:
            conv("w_dn1", wdn_in[1])

    conv_batch(0)

    ctmp, b_ctmp = P.sb("ctmp", [128, KC, 2], F32)
    dma("sp", lambda e: e.dma_start(out=ctmp[:], in_=cT_in), P.sem("misc"), reads=[b_in], writes=[b_ctmp])
    dma("sp", lambda e: e.dma_start(out=ngt[:], in_=ng_in), P.sem("misc"), reads=[b_in], writes=[b_ngt])
    dma("sp", lambda e: e.dma_start(out=umat[:], in_=umat_in), P.sem("misc"), reads=[b_in], writes=[b_umat])
    op("act", lambda e: e.activation(out=silc[:], in_=ctmp[:], func=AF.Silu), reads=[b_ctmp], writes=[b_silc])
    op("dve", lambda e: e.memset(cst[:, 0:1], EPS), writes=[b_cst])
    op("dve", lambda e: e.memset(cst[:, 1:2], 1.0), writes=[b_cst])
    op("dve", lambda e: e.memset(ones_bf[:], 1.0 / D), writes=[b_ones])
    op("dve", lambda e: e.memset(ones1_bf[:], 1.0), writes=[b_ones1])
    op("dve", lambda e: e.memset(ones256_bf[:], 1.0 / 256), writes=[b_ones256])
    op("dve", lambda e: e.tensor_copy(out=ident_b[:], in_=umat[:, 4, :]), reads=[b_umat], writes=[b_identb])
    P.phase_end()

    def mod_parts(l, parts):
        P.phase_begin()
        mt, b_mt = modv[l]
        mb, b_mb = P.sb("mb", [128, 96, 2], F32)
        dma("sp", lambda e: e.dma_start(out=mb[:], in_=modb_in[l]), P.sem("misc"), reads=[b_in], writes=[b_mb])
        wts = [P.sb(f"mw{i}", [128, KC, 128], BF16) + (P.sem(f"mws{l}_{parts[0]}_{i}"),) for i in range(4)]
        jobs = [(part, n) for part in parts for n in range(16)]

        def load(idx):
            part, n = jobs[idx]
            w, b_w, sw = wts[idx % 4]
            wt_d, b_wd = wbf[f"mod{l}_{part}"]
            dma("sp", lambda e: e.dma_start(out=w[:], in_=wt_d[n]), sw, reads=[b_wd], writes=[b_w])

        for idx in range(min(3, len(jobs))):
            load(idx)
        for idx, (part, n) in enumerate(jobs):
            if idx + 3 < len(jobs):
                load(idx + 3)
            w, b_w, sw = wts[idx % 4]
            pst, b_pst = ps[part % 2], b_ps[part % 2]
            for kc in range(KC):
                op("pe", lambda e, w=w, kc=kc, n=n, pst=pst: e.matmul(
                    out=pst[:, 2 * n:2 * n + 2], lhsT=w[:, kc, :], rhs=silc[:, kc, :],
                    start=(kc == 0), stop=(kc == KC - 1)),
                   reads=[b_w, b_silc], writes=[b_pst])
            if n == 15:
                op("dve", lambda e, part=part, pst=pst: e.tensor_tensor(
                    out=mt[:, part * 16:(part + 1) * 16, :],
                    in0=pst[:, 0:32].rearrange("p (n j) -> p n j", j=2),
                    in1=mb[:, part * 16:(part + 1) * 16, :], op=ALU.add),
                   reads=[b_pst, b_mb], writes=[b_mt])
                if part in (1, 4):
                    ni = 0 if part == 1 else 1
                    g, b_g = gsc[l]
                    for j in range(2):
                        op("dve", lambda e, part=part, ni=ni, j=j, g=g: e.scalar_tensor_tensor(
                            out=g[:, ni, :, j], in0=mt[:, part * 16:(part + 1) * 16, j], scalar=1.0,
                            in1=ngt[:, 2 * l + ni, :], op0=ALU.add, op1=ALU.mult),
                           reads=[b_mt, b_ngt], writes=[b_g])
        P.phase_end()

    def sumsq_rstd(xsrc, b_xsrc, s0, T, xch, sq, rstd, b_rstd, pss, ones_t, b_ones_t, nch, row0=0):
        pcs = pieces_of(T)
        for c in range(nch):
            x_t, b_x, sx = xch.next()
            dma("sp", lambda e, x_t=x_t, c=c: e.dma_start(
                out=x_t[:, 0:T], in_=xsrc[row0 + c * 128:row0 + (c + 1) * 128, s0:s0 + T]),
                sx, reads=[b_xsrc], writes=[b_x])
            q_t, b_q = sq.next()
            op("act", lambda e, x_t=x_t, q_t=q_t: e.activation(out=q_t[:, 0:T], in_=x_t[:, 0:T], func=AF.Square),
               reads=[b_x], writes=[b_q])
            for pi, (c0, pc) in enumerate(pcs):
                pst, b_pst = pss[pi]
                op("pe", lambda e, pst=pst, q_t=q_t, c0=c0, pc=pc, c=c: e.matmul(
                    out=pst[:, 0:pc], lhsT=ones_t[:], rhs=q_t[:, c0:c0 + pc], start=(c == 0), stop=(c == nch - 1)),
                   reads=[b_ones_t, b_q], writes=[b_pst])
        for pi, (c0, pc) in enumerate(pcs):
            pst, b_pst = pss[pi]
            op("act", lambda e, pst=pst, c0=c0, pc=pc: e.activation(
                out=rstd[:, c0:c0 + pc], in_=pst[:, 0:pc], func=AF.Sqrt, bias=cst[:, 0:1]),
               reads=[b_pst, b_cst], writes=[b_rstd])
            op("dve", lambda e, c0=c0, pc=pc: e.reciprocal(out=rstd[:, c0:c0 + pc], in_=rstd[:, c0:c0 + pc]),
               reads=[b_rstd], writes=[b_rstd])

    def norm_tile(xsrc, b_xsrc, s0, T, ranges, hT, b_hT, xch, sq, rstd, b_rstd, pss):
        sumsq_rstd(xsrc, b_xsrc, s0, T, xch, sq, rstd, b_rstd, pss, ones_bf, b_ones, KC)
        for c in range(KC):
            x_t, b_x, sx = xch.next()
            dma("sp", lambda e, x_t=x_t, c=c: e.dma_start(out=x_t[:, 0:T], in_=xsrc[c * 128:(c + 1) * 128, s0:s0 + T]),
                sx, reads=[b_xsrc], writes=[b_x])
            for (c0, c1, gfn, sfn) in ranges:
                op("dve", lambda e, x_t=x_t, c0=c0, c1=c1, gfn=gfn, c=c: e.scalar_tensor_tensor(
                    out=x_t[:, c0:c1], in0=x_t[:, c0:c1], scalar=gfn(c), in1=rstd[:, c0:c1],
                    op0=ALU.mult, op1=ALU.mult),
                   reads=[b_x, b_rstd], writes=[b_x])
                op("act", lambda e, x_t=x_t, c0=c0, c1=c1, sfn=sfn, c=c: e.activation(
                    out=hT[:, c, c0:c1], in_=x_t[:, c0:c1], func=AF.Identity, bias=sfn(c)),
                   reads=[b_x], writes=[b_hT])

    def mod_ranges(l, ni, s0, T):
        g, _ = gsc[l]
        mt, _ = modv[l]
        shp = 0 if ni == 0 else 3
        res = []
        a, b = s0, s0 + T
        if a < NCTX:
            res.append((0, min(b, NCTX) - a, (lambda c: g[:, ni, c, 1:2]), (lambda c: mt[:, shp * 16 + c, 1:2])))
        if b > NCTX:
            res.append((max(a, NCTX) - a, b - a, (lambda c: g[:, ni, c, 0:1]), (lambda c: mt[:, shp * 16 + c, 0:1])))
        return res

    def gate_ranges(l, gpart, s0, T):
        mt, _ = modv[l]
        res = []
        a, b = s0, s0 + T
        if a < NCTX:
            res.append((0, min(b, NCTX) - a, (lambda n: mt[:, gpart * 16 + n, 1:2])))
        if b > NCTX:
            res.append((max(a, NCTX) - a, b - a, (lambda n: mt[:, gpart * 16 + n, 0:1])))
        return res

    class WPool:
        def __init__(self, bufs):
            self.bufs = bufs
            self.nw = len(bufs)
            self.pos = 0
            self.q = []

        def load(self, key):
            wname, n = key
            wt_d, b_wd = wbf[wname]
            w, b_w, sw = self.bufs[self.pos % self.nw]
            self.pos += 1
            dma("sp", lambda e: e.dma_start(out=w[:], in_=wt_d[n]), sw, reads=[b_wd], writes=[b_w])
            self.q.append((key, w, b_w))

        def pop(self, key):
            k, w, b_w = self.q.pop(0)
            assert k == key, (k, key)
            return w, b_w

    def lin(wname, kcn, groups, act, b_act, T, epi, wts, psr, msl=None, nxt=()):
        pcs = pieces_of(T)
        flat = [(wname, n) for g in groups for n in g]
        upcoming = flat + list(nxt)
        ahead = wts.nw - 1
        npre = len(wts.q)
        assert [k for (k, _, _) in wts.q] == upcoming[:npre], ([k for (k, _, _) in wts.q], upcoming[:npre])
        state = {"issued": npre}

        def topup(consumed):
            while state["issued"] < len(upcoming) and state["issued"] < consumed + ahead:
                wts.load(upcoming[state["issued"]])
                state["issued"] += 1

        idx = 0
        topup(1)
        for g in groups:
            pls = []
            for n in g:
                topup(idx + 1)
                w, b_w = wts.pop((wname, n))
                idx += 1
                pl = [psr.next() for _ in pcs]
                for kc in range(kcn):
                    for (pst, b_pst), (c0, pc) in zip(pl, pcs):
                        if msl is None:
                            lhs = w[:, kc, :]
                            o = pst[:, 0:pc]
                        else:
                            lhs = w[:, kc, msl[0]:msl[1]]
                            o = pst[0:msl[1] - msl[0], 0:pc]
                        op("pe", lambda e, o=o, lhs=lhs, kc=kc, c0=c0, pc=pc: e.matmul(
                            out=o, lhsT=lhs, rhs=act[:, kc, c0:c0 + pc],
                            start=(kc == 0), stop=(kc == kcn - 1)),
                           reads=[b_w, b_act], writes=[b_pst])
                pls.append([(pst, b_pst, c0, pc) for (pst, b_pst), (c0, pc) in zip(pl, pcs)])
            epi(g, pls)
        topup(idx + 1)

    def lin_tok(wname, g0, ngroups, act, b_act, T, epi, wtk, psr):
        wt_d, b_wd = wbf[wname]
        for gi in range(ngroups):
            w, b_w, sw = wtk.next()
            dma("sp", lambda e, w=w, gi=gi: e.dma_start(
                out=w[:], in_=wt_d[g0 + gi * 4:g0 + gi * 4 + 4].rearrange("g p k j -> p g k j")),
                sw, reads=[b_wd], writes=[b_w])
            for tb in range(T // 128):
                pst, b_pst = psr.next()
                for kc in range(KC):
                    op("pe", lambda e, pst=pst, w=w, kc=kc, tb=tb: e.matmul(
                        out=pst[:, :].rearrange("p (g j) -> p g j", j=128), lhsT=act[:, kc, tb * 128:(tb + 1) * 128],
                        rhs=w[:, :, kc, :], start=(kc == 0), stop=(kc == KC - 1)),
                       reads=[b_w, b_act], writes=[b_pst])
                epi(gi, tb, pst, b_pst)

    def resid_epi(l, gpart, s0, T, xsrc, b_xsrc, xdst, b_xdst, xo, d0=None):
        granges = gate_ranges(l, gpart, s0, T)
        if d0 is None:
            d0 = s0

        def epi(g, pls):
            n = g[0]
            pl = pls[0]
            x_t, b_x, sx = xo.next()
            dma("sp", lambda e: e.dma_start(out=x_t[:, 0:T], in_=xsrc[n * 128:(n + 1) * 128, s0:s0 + T]),
                sx, reads=[b_xsrc], writes=[b_x])
            for (pst, b_pst, c0, pc) in pl:
                for (r0, r1, gfn) in granges:
                    lo, hi = max(r0, c0), min(r1, c0 + pc)
                    if lo >= hi:
                        continue
                    op("dve", lambda e, pst=pst, lo=lo, hi=hi, c0=c0, gfn=gfn: e.scalar_tensor_tensor(
                        out=x_t[:, lo:hi], in0=pst[:, lo - c0:hi - c0], scalar=gfn(n), in1=x_t[:, lo:hi],
                        op0=ALU.mult, op1=ALU.add),
                       reads=[b_pst, b_x], writes=[b_x])
            dma("sp", lambda e: e.dma_start(out=xdst[n * 128:(n + 1) * 128, d0:d0 + T], in_=x_t[:, 0:T]),
                sx, reads=[b_x], writes=[b_xdst])
        return epi

    def mlp_phase(l, xsrc, b_xsrc, xdst, b_xdst, tiles):
        P.phase_begin()
        TM = max(t for _, t in tiles)
        if l == 0:
            conv_batch(5)
        hTs = [P.sb(f"hT{i}", [128, KC, TM], BF16) for i in range(2)]
        aT, b_aT = P.sb("aT", [128, 64, TM], BF16)
        rstds = [P.sb(f"rstd{i}", [128, TM], F32) for i in range(2)]
        xch = Rot([P.sb(f"xch{i}", [128, TM], F32) + (P.sem(f"mlp{l}_x{i}"),) for i in range(4)])
        sq = Rot(P.sbn("sq", [128, TM], BF16, 2))
        wup = WPool([P.sb(f"wup{i}", [128, KC, 128], BF16) + (P.sem(f"mlp{l}_wu{i}"),) for i in range(3)])
        wdn = WPool([P.sb(f"wdn{i}", [128, 64, 128], BF16) + (P.sem(f"mlp{l}_wd{i}"),) for i in range(2)])
        rl = Rot(P.sbn("rl", [128, TM], F32, 2))
        psr = Rot(list(zip(ps, b_ps)))
        def do_norm(i):
            s0, T = tiles[i]
            hT, b_hT = hTs[i % 2]
            rstd, b_rstd = rstds[i % 2]
            norm_tile(xsrc, b_xsrc, s0, T, mod_ranges(l, 1, s0, T), hT, b_hT, xch, sq, rstd, b_rstd,
                      [psr.next(), psr.next()])

        do_norm(0)
        for ti, (s0, T) in enumerate(tiles):
            if ti + 1 < len(tiles):
                do_norm(ti + 1)
            hT, b_hT = hTs[ti % 2]

            def epi_up(g, pls, T=T):
                n = g[0]
                r_t, b_r = rl.next()
                for (pst, b_pst, c0, pc) in pls[0]:
                    op("act", lambda e, pst=pst, c0=c0, pc=pc: e.activation(
                        out=r_t[:, c0:c0 + pc], in_=pst[:, 0:pc], func=AF.Relu), reads=[b_pst], writes=[b_r])
                op("dve", lambda e: e.tensor_tensor(
                    out=aT[:, n, 0:T], in0=r_t[:, 0:T], in1=r_t[:, 0:T], op=ALU.mult), reads=[b_r], writes=[b_aT])

            last = ti + 1 == len(tiles)
            if ti == 0:
                wdn.load((f"w_dn{l}", 0))
            lin(f"w_up{l}", KC, [[n] for n in range(64)], hT, b_hT, T, epi_up, wup, psr,
                nxt=[] if last else [(f"w_up{l}", n) for n in range(2)])
            lin(f"w_dn{l}", 64, [[n] for n in range(16)], aT, b_aT, T,
                resid_epi(l, 5, s0, T, xsrc, b_xsrc, xdst, b_xdst, xch), wdn, psr,
                nxt=[] if last else [(f"w_dn{l}", 0)])
        P.phase_end()

    def proj_resid_phase(tag, l, wname, asrc, b_asrc, a0_of, xsrc, b_xsrc, xdst, b_xdst, tiles):
        P.phase_begin()
        TM = max(t for _, t in tiles)
        if tag == "l0d":
            conv_batch(4)
        aTt = Rot([P.sb(f"pa{i}", [128, KC, TM], BF16) + (P.sem(f"{tag}_a{i}"),) for i in range(2)])
        xo = Rot([P.sb(f"pxo{i}", [128, TM], F32) + (P.sem(f"{tag}_x{i}"),) for i in range(4)])
        wts = WPool([P.sb(f"pw{i}", [128, KC, 128], BF16) + (P.sem(f"{tag}_w{i}"),) for i in range(6)])
        psr = Rot(list(zip(ps, b_ps)))
        def load_act(ti):
            s0, T = tiles[ti]
            a_t, b_a, sa = aTt.next()
            a0 = a0_of(s0)
            dma("sp", lambda e: e.dma_start(
                out=a_t[:, :, 0:T], in_=asrc[:, a0:a0 + T].rearrange("(c p) t -> p c t", p=128)),
                sa, reads=[b_asrc], writes=[b_a])
            return a_t, b_a

        acts = {0: load_act(0)}
        for ti, (s0, T) in enumerate(tiles):
            if ti + 1 < len(tiles):
                acts[ti + 1] = load_act(ti + 1)
            a_t, b_a = acts[ti]
            lin(wname, KC, [[n] for n in range(16)], a_t, b_a, T,
                resid_epi(l, 2, s0, T, xsrc, b_xsrc, xdst, b_xdst, xo), wts, psr,
                nxt=[] if ti + 1 == len(tiles) else [(wname, n) for n in range(2)])
        P.phase_end()

    mod_parts(0, [0, 1])

    def l0a():
        P.phase_begin()
        TM = 896
        hTs = [P.sb(f"hT{i}", [128, KC, TM], BF16) for i in range(2)]
        rstds = [P.sb(f"rstd{i}", [128, TM], F32) for i in range(2)]
        xch = Rot([P.sb(f"xch{i}", [128, TM], F32) + (P.sem(f"a_x{i}"),) for i in range(3)])
        sq = Rot(P.sbn("sq", [128, TM], BF16, 2))
        wts = WPool([P.sb(f"w{i}", [128, KC, 128], BF16) + (P.sem(f"a_w{i}"),) for i in range(4)])
        wtk = Rot([P.sb(f"wk{i}", [128, 4, KC, 128], BF16) + (P.sem(f"a_wk{i}"),) for i in range(2)])
        assert True
        rope = [P.sb(f"rope{i}", [128, TM], F32) + (P.sem(f"a_rope{i}"),) for i in range(4)]
        ob = Rot([P.sb(f"ob{i}", [128, TM], F32) + (P.sem(f"a_ob{i}"),) for i in range(3)])
        obh = Rot([P.sb(f"obh{i}", [128, TM], BF16) + (P.sem(f"a_obh{i}"),) for i in range(3)])
        t1r = Rot(P.sbn("t1_", [128, TM], F32, 1))
        t2r = Rot(P.sbn("t2_", [128, TM], F32, 1))
        vob = Rot([P.sb(f"vob{i}", [128, 512], BF16) + (P.sem(f"a_vob{i}"),) for i in range(3)])
        glw = [P.sb(f"glw{i}", [17, TM], F32) for i in range(2)]
        gup, b_gup = P.sb("gup", [17, 2, 512], F32)
        et = Rot(P.sbn("et", [128, 512], F32, 2))
        lt = Rot([P.sb(f"lt{i}", [128, 512], F32) + (P.sem(f"a_lt{i}"),) for i in range(3)])
        psr = Rot(list(zip(ps, b_ps)))
        dma("sp", lambda e: e.dma_start(out=gup[:], in_=gup_in), P.sem("misc"), reads=[b_in], writes=[b_gup])
        for (g_t, b_g) in glw:
            op("dve", lambda e, g_t=g_t: e.memset(g_t[:], 1.0), writes=[b_g])

        def do_norm(s0, T, slot):
            hT, b_hT = hTs[slot]
            rstd, b_rstd = rstds[slot]
            norm_tile(xT_in, b_in, s0, T, mod_ranges(0, 0, s0, T), hT, b_hT, xch, sq, rstd, b_rstd,
                      [psr.next(), psr.next()])

        def do_tile(s0, T, slot, nxt_tile):
            full = s0 < NOUT
            hT, b_hT = hTs[slot]
            for i in range(4):
                if i < 2 and not full:
                    continue
                r_t, b_r, s_r = rope[i]
                dma("sp", lambda e, r_t=r_t, i=i: e.dma_start(out=r_t[:, 0:T], in_=rope_in[i, :, s0:s0 + T]),
                    s_r, reads=[b_in], writes=[b_r])

            def rope_epi(dst, b_dst, ci, si, base):
                (c_t, b_c, _), (s_t, b_s, _) = rope[ci], rope[si]

                def epi(g, pls):
                    h = g[0] - base
                    o_t, b_o, so = obh.next()
                    t1, b_t1 = t1r.next()
                    t2, b_t2 = t2r.next()
                    for (pa, b_pa, c0, pc), (pb, b_pb, _, _) in zip(pls[0], pls[1]):
                        op("dve", lambda e, pa=pa, c0=c0, pc=pc: e.tensor_tensor(
                            out=t1[:, c0:c0 + pc], in0=pa[:, 0:pc], in1=c_t[:, c0:c0 + pc], op=ALU.mult),
                           reads=[b_pa, b_c], writes=[b_t1])
                        op("dve", lambda e, pb=pb, c0=c0, pc=pc: e.tensor_tensor(
                            out=t2[:, c0:c0 + pc], in0=pb[:, 0:pc], in1=s_t[:, c0:c0 + pc], op=ALU.mult),
                           reads=[b_pb, b_s], writes=[b_t2])
                    op("dve", lambda e: e.tensor_tensor(out=o_t[:, 0:T], in0=t1[:, 0:T], in1=t2[:, 0:T], op=ALU.add),
                       reads=[b_t1, b_t2], writes=[b_o])
                    for sp in range(s0 // 256, (s0 + T - 1) // 256 + 1):
                        lo, hi = max(s0, sp * 256), min(s0 + T, sp * 256 + 256)
                        dma("sp", lambda e, sp=sp, lo=lo, hi=hi: e.dma_start(
                            out=dst[sp, :, h, lo - sp * 256:hi - sp * 256], in_=o_t[:, lo - s0:hi - s0]),
                            so, reads=[b_o], writes=[b_dst])
                return epi

            def act_epi(dst, b_dst, base, func, scale=1.0):
                def epi(g, pls):
                    n = g[0] - base
                    o_t, b_o, so = ob.next()
                    for (pst, b_pst, c0, pc) in pls[0]:
                        op("act", lambda e, pst=pst, c0=c0, pc=pc: e.activation(
                            out=o_t[:, c0:c0 + pc], in_=pst[:, 0:pc], func=func, scale=scale),
                           reads=[b_pst], writes=[b_o])
                    dma("sp", lambda e: e.dma_start(out=dst[n * 128:(n + 1) * 128, s0:s0 + T], in_=o_t[:, 0:T]),
                        so, reads=[b_o], writes=[b_dst])
                return epi

            def v_epi(gi, tb, pst, b_pst):
                o_t, b_o, so = vob.next()
                op("act", lambda e: e.activation(out=o_t[:], in_=pst[:], func=AF.Copy), reads=[b_pst], writes=[b_o])
                tok = s0 + tb * 128
                dma("sp", lambda e: e.dma_start(
                    out=vS[tok // 256, :, (tok // 128) % 2, gi * 512:(gi + 1) * 512], in_=o_t[:]),
                    so, reads=[b_o], writes=[b_vS])


            seq = []
            if full:
                seq.append(dict(groups=[[h, 4 + h] for h in range(4)], epi=rope_epi(qT, b_qT, 0, 1, 0)))
            seq.append(dict(groups=[[8 + h, 12 + h] for h in range(4)], epi=rope_epi(kT, b_kT, 2, 3, 8)))
            seq.append("v")
            if full:
                seq.append(dict(groups=[[24 + n] for n in range(8)], epi=act_epi(ogT, b_ogT, 24, AF.Silu)))
            seq.append(dict(groups=[[32 + n] for n in range(8)], epi=act_epi(xrT, b_xrT, 32, AF.Copy)))
            if full:
                seq.append(dict(groups=[[40 + n] for n in range(8)], epi=act_epi(gyT, b_gyT, 40, AF.Gelu)))
            for di in range(2):
                g_t, b_g = glw[di]

                def gl_epi(g, pls, g_t=g_t, b_g=b_g):
                    for (pst, b_pst, c0, pc) in pls[0]:
                        op("act", lambda e, pst=pst, c0=c0, pc=pc, g_t=g_t: e.activation(
                            out=g_t[0:16, c0:c0 + pc], in_=pst[0:16, 0:pc], func=AF.Copy),
                           reads=[b_pst], writes=[b_g])

                seq.append(dict(groups=[[48]], epi=gl_epi, msl=(16 * di, 16 * di + 16)))
            for i, cdef in enumerate(seq):
                if cdef == "v":
                    lin_tok("w_in", 16, 2, hT, b_hT, T, v_epi, wtk, psr)
                    continue
                nk = list(nxt_tile)
                for later in seq[i + 1:]:
                    if later != "v":
                        nk = [("w_in", n) for g in later["groups"] for n in g][:3]
                        break
                lin("w_in", KC, cdef["groups"], hT, b_hT, T, cdef["epi"], wts, psr, msl=cdef.get("msl"), nxt=nk)
            for di in range(2):
                g_t, b_g = glw[di]
                for tb in range(T // 128):
                    pst, b_pst = psr.next()
                    op("pe", lambda e, pst=pst, g_t=g_t, tb=tb, di=di: e.matmul(
                        out=pst[:, :], lhsT=g_t[0:17, tb * 128:(tb + 1) * 128], rhs=gup[0:17, di, :],
                        start=True, stop=True), reads=[b_g, b_gup], writes=[b_pst])
                    e_t, b_e = et.next()
                    l_t, b_l, sl = lt.next()
                    op("act", lambda e, pst=pst, e_t=e_t: e.activation(out=e_t[:], in_=pst[:], func=AF.Exp, scale=-1.0),
                       reads=[b_pst], writes=[b_e])
                    op("act", lambda e, l_t=l_t, e_t=e_t: e.activation(out=l_t[:], in_=e_t[:], func=AF.Ln, bias=cst[:, 1:2]),
                       reads=[b_e], writes=[b_l])
                    tok = s0 + tb * 128
                    dma("sp", lambda e, l_t=l_t, tok=tok, di=di: e.dma_start(
                        out=LS[di, tok // 256, :, (tok // 128) % 2, :], in_=l_t[:]),
                        sl, reads=[b_l], writes=[b_LS])

        tl = L0_TILES + L0_OTHER
        do_norm(tl[0][0], tl[0][1], 0)
        for ti, (s0_, T_) in enumerate(tl):
            if ti + 1 < len(tl):
                do_norm(tl[ti + 1][0], tl[ti + 1][1], (ti + 1) % 2)
            if ti + 1 == len(tl):
                nt = []
            elif tl[ti + 1][0] < NOUT:
                nt = [("w_in", 0), ("w_in", 4), ("w_in", 1)]
            else:
                nt = [("w_in", 8), ("w_in", 12), ("w_in", 9)]
            do_tile(s0_, T_, ti % 2, nt)
        P.phase_end()

    if upto >= 1:
        l0a()

    def l0b():
        P.phase_begin()
        conv_batch(1)
        conv_batch(2)
        oacc, b_oacc = P.sb("oacc", [128, 8, NOUT], F32)
        b_oaccs = [Buf(f"oacc_blk{i}") for i in range(NOUT // 128)]
        for i in range(NOUT // 128):
            op("dve", lambda e, i=i: e.memset(oacc[:, :, i * 128:(i + 1) * 128], 0.0), writes=[b_oaccs[i]])
        Sst = [[P.sb(f"S{d}{h}", [128, 256], F32) for h in range(4)] for d in range(2)]
        Sbf = [[P.sb(f"Sb{d}{h}", [128, 256], BF16) for h in range(4)] for d in range(2)]
        qsp = Rot([P.sb(f"qsp{i}", [128, 4, 256], BF16) + (P.sem(f"b_q{i}"),) for i in range(6)])
        ksp = Rot([P.sb(f"ksp{i}", [128, 4, 256], BF16) + (P.sem(f"b_k{i}"),) for i in range(6)])
        vsp = Rot([P.sb(f"vsp{i}", [128, 2, 1024], BF16) + (P.sem(f"b_v{i}"),) for i in range(6)])
        lsp = Rot([P.sb(f"lsp{i}", [128, 2, 512], F32) + (P.sem(f"b_l{i}"),) for i in range(6)])
        NR = 6
        eb_r = Rot(P.sbn("eb", [128, 128], F32, NR))
        enb_r = Rot(P.sbn("enb", [128, 128], F32, NR))
        kt_r = Rot(P.sbn("ktl", [128, 128], BF16, NR))
        qt_r = Rot(P.sbn("qtl", [128, 128], BF16, NR))
        kh_r = Rot(P.sbn("khT", [128, 128], BF16, NR))
        khs_r = Rot(P.sbn("khs", [128, 128], BF16, NR))
        am_r = Rot(P.sbn("attm", [128, 128], BF16, NR))
        psAA = [[(ps[4 * d], b_ps[4 * d]), (ps[4 * d + 1], b_ps[4 * d + 1])] for d in range(2)]
        psO = [(ps[4 * d + 2], b_ps[4 * d + 2]) for d in range(2)]
        psD = [(ps[4 * d + 3], b_ps[4 * d + 3]) for d in range(2)]
        for d in range(2):
            for h in range(4):
                s_t, b_s = Sst[d][h]
                sb_t, b_sb = Sbf[d][h]
                op("dve", lambda e, s_t=s_t: e.memset(s_t[:], 0.0), writes=[b_s])
                op("dve", lambda e, sb_t=sb_t: e.memset(sb_t[:], 0.0), writes=[b_sb])

        def spans(d):
            if d == 0:
                return [(0, True)] + [(NCTX + 256 * i, True) for i in range(9)]
            return [(0, True)] + [(NCTX + 256 * i, i < 9) for i in range(15, -1, -1)]

        def unit_front(d, uidx, full, blk, h, t0, k_t, b_k, v_t, b_v, l_t, b_l, q_t, b_q):
            pa, b_pa = psAA[d][uidx % 2]
            eb, b_eb = eb_r.next()
            enb, b_enb = enb_r.next()
            ktl, b_ktl = kt_r.next()
            khT, b_khT = kh_r.next()
            khs, b_khs = khs_r.next()
            c = dict(d=d, full=full, blk=blk, h=h, t0=t0, v_t=v_t, b_v=b_v, eb=eb, b_eb=b_eb, khs=khs, b_khs=b_khs)
            op("pe", lambda e: e.matmul(
                out=pa[:, 0:128], lhsT=l_t[:, blk, h * 128:(h + 1) * 128], rhs=umat[:, d, :],
                start=True, stop=True), reads=[b_l, b_umat], writes=[b_pa])
            op("act", lambda e: e.activation(out=eb[:], in_=pa[:, 0:128], func=AF.Exp),
               reads=[b_pa], writes=[b_eb])
            op("act", lambda e: e.activation(out=enb[:], in_=pa[:, 0:128], func=AF.Exp, scale=-1.0),
               reads=[b_pa], writes=[b_enb])
            op("dve", lambda e: e.tensor_tensor(
                out=ktl[:], in0=k_t[:, h, blk * 128:(blk + 1) * 128], in1=enb[:], op=ALU.mult),
               reads=[b_k, b_enb], writes=[b_ktl])
            for cc in range(2):
                lc = cc * 64 + (63 if d == 0 else 0)
                op("act", lambda e, cc=cc, lc=lc: e.activation(
                    out=khT[:, cc * 64:(cc + 1) * 64], in_=ktl[:, cc * 64:(cc + 1) * 64],
                    func=AF.Copy, scale=eb[:, lc:lc + 1]),
                   reads=[b_ktl, b_eb], writes=[b_khT])
            op("pe", lambda e: e.matmul(
                out=pa[:, 128:256], lhsT=khT[:], rhs=ident_b[:], start=True, stop=True),
               reads=[b_khT, b_identb], writes=[b_pa])
            op("act", lambda e: e.activation(out=khs[:], in_=pa[:, 128:256], func=AF.Copy),
               reads=[b_pa], writes=[b_khs])
            if full:
                qtl, b_qtl = qt_r.next()
                attm, b_am = am_r.next()
                c.update(qtl=qtl, b_qtl=b_qtl, attm=attm, b_am=b_am)
                op("dve", lambda e: e.tensor_tensor(
                    out=qtl[:], in0=q_t[:, h, blk * 128:(blk + 1) * 128], in1=eb[:], op=ALU.mult),
                   reads=[b_q, b_eb], writes=[b_qtl])
                op("pe", lambda e: e.matmul(
                    out=pa[:, 256:384], lhsT=ktl[:], rhs=qtl[:], start=True, stop=True),
                   reads=[b_ktl, b_qtl], writes=[b_pa])
                op("dve", lambda e: e.tensor_tensor(
                    out=attm[:], in0=pa[:, 256:384], in1=umat[:, 2 + d, :], op=ALU.mult),
                   reads=[b_pa, b_umat], writes=[b_am])
            return c

        def unit_back(c):
            d, full, blk, h, t0 = c["d"], c["full"], c["blk"], c["h"], c["t0"]
            v_t, b_v, eb, b_eb, khs, b_khs = c["v_t"], c["b_v"], c["eb"], c["b_eb"], c["khs"], c["b_khs"]
            po, b_po = psO[d]
            pd, b_pd = psD[d]
            s_t, b_s = Sst[d][h]
            sb_t, b_sb = Sbf[d][h]
            if full:
                qtl, b_qtl, attm, b_am = c["qtl"], c["b_qtl"], c["attm"], c["b_am"]
                for j in range(2):
                    op("pe", lambda e, j=j: e.matmul(
                        out=po[:, j * 128:(j + 1) * 128],
                        lhsT=v_t[:, blk, h * 256 + j * 128:h * 256 + (j + 1) * 128], rhs=attm[:],
                        start=True, stop=True), reads=[b_v, b_am], writes=[b_po])
            corder = [0, 1] if d == 0 else [1, 0]
            for ci, cc in enumerate(corder):
                lc = cc * 64 + (63 if d == 0 else 0)
                if full:
                    for j in range(2):
                        op("pe", lambda e, j=j, cc=cc: e.matmul(
                            out=po[:, 256 + j * 128 + cc * 64:256 + j * 128 + (cc + 1) * 64],
                            lhsT=sb_t[:, j * 128:(j + 1) * 128], rhs=qtl[:, cc * 64:(cc + 1) * 64],
                            start=True, stop=True), reads=[b_sb, b_qtl], writes=[b_po])
                op("pe", lambda e, cc=cc: e.matmul(
                    out=pd[:, 0:256], lhsT=khs[cc * 64:(cc + 1) * 64, :],
                    rhs=v_t[cc * 64:(cc + 1) * 64, blk, h * 256:(h + 1) * 256], start=True, stop=True),
                   reads=[b_khs, b_v], writes=[b_pd])
                op("dve", lambda e, lc=lc: e.scalar_tensor_tensor(
                    out=s_t[:], in0=s_t[:], scalar=eb[:, lc:lc + 1], in1=pd[:, 0:256],
                    op0=ALU.mult, op1=ALU.add), reads=[b_s, b_eb, b_pd], writes=[b_s])
                op("act", lambda e: e.activation(out=sb_t[:], in_=s_t[:], func=AF.Copy),
                   reads=[b_s], writes=[b_sb])
                if ci == 0:
                    yield
            if full:
                b_oa = b_oaccs[t0 // 128]
                for j in range(2):
                    dst = oacc[:, h * 2 + j, t0:t0 + 128]
                    for part in range(2):
                        op("dve", lambda e, dst=dst, j=j, part=part: e.tensor_tensor(
                            out=dst, in0=dst, in1=po[:, part * 256 + j * 128:part * 256 + (j + 1) * 128], op=ALU.add),
                           reads=[b_po, b_oa], writes=[b_oa])

        def dir_gen(d):
            prev = None
            uidx = 0
            for (sp0, full) in spans(d):
                k_t, b_k, sk = ksp.next()
                dma("sp", lambda e, k_t=k_t, sp0=sp0: e.dma_start(out=k_t[:], in_=kT[sp0 // 256]),
                    sk, reads=[b_kT], writes=[b_k])
                v_t, b_v, sv = vsp.next()
                dma("sp", lambda e, v_t=v_t, sp0=sp0: e.dma_start(out=v_t[:], in_=vS[sp0 // 256]),
                    sv, reads=[b_vS], writes=[b_v])
                l_t, b_l, sl = lsp.next()
                dma("sp", lambda e, l_t=l_t, sp0=sp0: e.dma_start(out=l_t[:], in_=LS[d, sp0 // 256]),
                    sl, reads=[b_LS], writes=[b_l])
                q_t = b_q = None
                if full:
                    q_t, b_q, sq_ = qsp.next()
                    dma("sp", lambda e, q_t=q_t, sp0=sp0: e.dma_start(out=q_t[:], in_=qT[sp0 // 256]),
                        sq_, reads=[b_qT], writes=[b_q])
                for blk in ([0, 1] if d == 0 else [1, 0]):
                    t0 = sp0 + blk * 128
                    for h in range(4):
                        c = unit_front(d, uidx, full, blk, h, t0, k_t, b_k, v_t, b_v, l_t, b_l, q_t, b_q)
                        uidx += 1
                        if prev is not None:
                            yield from unit_back(prev)
                            yield
                        prev = c
            yield from unit_back(prev)
            yield

        gens = [dir_gen(0), dir_gen(1)]
        while gens:
            for g_ in list(gens):
                try:
                    next(g_)
                except StopIteration:
                    gens.remove(g_)

        glag, b_glag = P.sb("glag", [128, 8], F32)
        dma("sp", lambda e: e.dma_start(out=glag[:], in_=glag_in), P.sem("misc"), reads=[b_in], writes=[b_glag])
        sqr = Rot(P.sbn("gsq", [128, 512], BF16, 3))
        rs_r = Rot(P.sbn("grs", [128, 512], F32, 2))
        sg_r = Rot([P.sb(f"gsg{i}", [128, 512], F32) + (P.sem(f"b_sg{i}"),) for i in range(3)])
        tm_r = Rot(P.sbn("gtm", [128, 512], F32, 2))
        go_r = Rot([P.sb(f"ggo{i}", [128, 512], BF16) + (P.sem(f"b_go{i}"),) for i in range(3)])
        psr = Rot(list(zip(ps, b_ps)))
        for h in range(4):
            for c0 in range(0, NOUT, 512):
                pst, b_pst = psr.next()
                for j in range(2):
                    q_t, b_q = sqr.next()
                    op("act", lambda e, q_t=q_t, h=h, j=j, c0=c0: e.activation(
                        out=q_t[:], in_=oacc[:, h * 2 + j, c0:c0 + 512], func=AF.Square),
                       reads=b_oaccs[c0 // 128:c0 // 128 + 4], writes=[b_q])
                    op("pe", lambda e, pst=pst, q_t=q_t, j=j: e.matmul(
                        out=pst[:], lhsT=ones256_bf[:], rhs=q_t[:], start=(j == 0), stop=(j == 1)),
                       reads=[b_ones256, b_q], writes=[b_pst])
                rs, b_rs = rs_r.next()
                op("act", lambda e, rs=rs, pst=pst: e.activation(out=rs[:], in_=pst[:], func=AF.Sqrt, bias=cst[:, 0:1]),
                   reads=[b_pst, b_cst], writes=[b_rs])
                op("dve", lambda e, rs=rs: e.reciprocal(out=rs[:], in_=rs[:]), reads=[b_rs], writes=[b_rs])
                for j in range(2):
                    n = h * 2 + j
                    sg, b_sg, ssg = sg_r.next()
                    dma("sp", lambda e, sg=sg, n=n, c0=c0: e.dma_start(out=sg[:], in_=ogT[n * 128:(n + 1) * 128, c0:c0 + 512]),
                        ssg, reads=[b_ogT], writes=[b_sg])
                    tm, b_tm = tm_r.next()
                    go, b_go, sgo = go_r.next()
                    op("dve", lambda e, tm=tm, n=n, c0=c0, rs=rs: e.scalar_tensor_tensor(
                        out=tm[:], in0=oacc[:, n, c0:c0 + 512], scalar=glag[:, n:n + 1], in1=rs[:],
                        op0=ALU.mult, op1=ALU.mult), reads=b_oaccs[c0 // 128:c0 // 128 + 4] + [b_glag, b_rs], writes=[b_tm])
                    op("dve", lambda e, go=go, tm=tm, sg=sg: e.tensor_tensor(out=go[:], in0=tm[:], in1=sg[:], op=ALU.mult),
                       reads=[b_tm, b_sg], writes=[b_go])
                    dma("sp", lambda e, go=go, n=n, c0=c0: e.dma_start(out=mixT[n * 128:(n + 1) * 128, c0:c0 + 512], in_=go[:]),
                        sgo, reads=[b_go], writes=[b_mixT])
        P.phase_end()

    if upto >= 2:
        l0b()

    def l0c():
        P.phase_begin()
        conv_batch(3)
        lw32, b_lw32 = P.sb("lw32", [128, 32, 128], F32)
        lw, b_lw = P.sb("lw", [128, 32, 128], BF16)
        lb, b_lb = P.sb("lb", [128, 32], F32)
        lam, b_lam = P.sb("lam", [128, 16], F32)
        cl, b_cl = P.sb("cl", [128, 16], F32)
        cw, b_cw = P.sb("cw", [128, 8, 6], F32)
        dma("sp", lambda e: e.dma_start(out=lw32[:], in_=lruw_in), P.sem("misc"), reads=[b_in], writes=[b_lw32])
        dma("sp", lambda e: e.dma_start(out=lb[:], in_=lrub_in), P.sem("misc"), reads=[b_in], writes=[b_lb])
        dma("sp", lambda e: e.dma_start(out=lam[:], in_=lam_in), P.sem("misc"), reads=[b_in], writes=[b_lam])
        dma("sp", lambda e: e.dma_start(out=cw[:], in_=conv_in), P.sem("misc"), reads=[b_in], writes=[b_cw])
        op("dve", lambda e: e.tensor_copy(out=lw[:], in_=lw32[:]), reads=[b_lw32], writes=[b_lw])
        op("act", lambda e: e.activation(out=cl[:], in_=lam[:], func=AF.Exp, scale=-1.0), reads=[b_lam], writes=[b_cl])
        op("act", lambda e: e.activation(out=cl[:], in_=cl[:], func=AF.Ln, bias=cst[:, 1:2]), reads=[b_cl], writes=[b_cl])
        op("dve", lambda e: e.tensor_scalar(out=cl[:], in0=cl[:], scalar1=-8.0, scalar2=None, op0=ALU.mult),
           reads=[b_cl], writes=[b_cl])
        xr_r = Rot([P.sb(f"xr{i}", [128, SEQ], F32) + (P.sem(f"c_xr{i}"),) for i in range(1)])
        xc, b_xc = P.sb("xc", [128, SEQ], F32)
        xcb, b_xcb = P.sb("xcb", [128, SEQ], BF16)
        rr, b_rr = P.sb("rr", [128, SEQ], F32)
        ii, b_ii = P.sb("ii", [128, SEQ], F32)
        a2, b_a2 = P.sb("a2", [128, SEQ], F32)
        hh = [P.sb(f"hh{d}", [128, NOUT if d == 0 else SEQ], F32) for d in range(2)]
        gy_r = Rot([P.sb(f"gy{i}", [128, NOUT], F32) + (P.sem(f"c_gy{i}"),) for i in range(1)])
        bo_r = Rot([P.sb(f"bo{i}", [128, NOUT], BF16) + (P.sem(f"c_bo{i}"),) for i in range(1)])
        psr = Rot(list(zip(ps, b_ps)))
        segs = [(0, NCTX), (NCTX, SEQ)]
        for g in range(8):
            xr, b_xr, sxr = xr_r.next()
            dma("sp", lambda e, xr=xr, g=g: e.dma_start(out=xr[:], in_=xrT[g * 128:(g + 1) * 128, :]),
                sxr, reads=[b_xrT], writes=[b_xr])
            gy, b_gy, sgy = gy_r.next()
            dma("sp", lambda e, gy=gy, g=g: e.dma_start(out=gy[:], in_=gyT[g * 128:(g + 1) * 128, :]),
                sgy, reads=[b_gyT], writes=[b_gy])
            op("act", lambda e, xr=xr, g=g: e.activation(
                out=xc[:], in_=xr[:], func=AF.Identity, scale=cw[:, g, 2:3], bias=cw[:, g, 5:6]),
               reads=[b_xr, b_cw], writes=[b_xc])
            for (a, b) in segs:
                for k, o in enumerate([-2, -1, 1, 2]):
                    tap = o + 2
                    lo, hi = max(a, a - o), min(b, b - o)
                    op("dve", lambda e, xr=xr, g=g, tap=tap, lo=lo, hi=hi, o=o: e.scalar_tensor_tensor(
                        out=xc[:, lo:hi], in0=xr[:, lo + o:hi + o], scalar=cw[:, g, tap:tap + 1], in1=xc[:, lo:hi],
                        op0=ALU.mult, op1=ALU.add), reads=[b_xr, b_cw, b_xc], writes=[b_xc])
            op("act", lambda e: e.activation(out=xcb[:], in_=xc[:], func=AF.Copy), reads=[b_xc], writes=[b_xcb])
            for d in range(2):
                N = NOUT if d == 0 else SEQ
                h_t, b_h = hh[d]
                groups = [(c0, min(512, N - c0)) for c0 in range(0, N, 512)]
                for gate, (dst, b_dst) in enumerate([(rr, b_rr), (ii, b_ii)]):
                    wi = d * 16 + gate * 8 + g
                    for (c0, pc) in groups:
                        pst, b_pst = psr.next()
                        op("pe", lambda e, pst=pst, wi=wi, c0=c0, pc=pc: e.matmul(
                            out=pst[:, 0:pc], lhsT=lw[:, wi, :], rhs=xcb[:, c0:c0 + pc], start=True, stop=True),
                           reads=[b_lw, b_xcb], writes=[b_pst])
                        op("act", lambda e, pst=pst, dst=dst, wi=wi, c0=c0, pc=pc: e.activation(
                            out=dst[:, c0:c0 + pc], in_=pst[:, 0:pc], func=AF.Sigmoid, bias=lb[:, wi:wi + 1]),
                           reads=[b_pst, b_lb], writes=[b_dst])
                op("act", lambda e, d=d, g=g, N=N: e.activation(
                    out=rr[:, 0:N], in_=rr[:, 0:N], func=AF.Exp, scale=cl[:, d * 8 + g:d * 8 + g + 1]),
                   reads=[b_rr, b_cl], writes=[b_rr])
                op("act", lambda e, N=N: e.activation(out=a2[:, 0:N], in_=rr[:, 0:N], func=AF.Square),
                   reads=[b_rr], writes=[b_a2])
                op("act", lambda e, N=N: e.activation(out=a2[:, 0:N], in_=a2[:, 0:N], func=AF.Sqrt, scale=-1.0, bias=cst[:, 1:2]),
                   reads=[b_a2], writes=[b_a2])
                op("dve", lambda e, N=N: e.tensor_tensor(out=ii[:, 0:N], in0=ii[:, 0:N], in1=xc[:, 0:N], op=ALU.mult),
                   reads=[b_ii, b_xc], writes=[b_ii])
                op("dve", lambda e, N=N: e.tensor_tensor(out=ii[:, 0:N], in0=ii[:, 0:N], in1=a2[:, 0:N], op=ALU.mult),
                   reads=[b_ii, b_a2], writes=[b_ii])
                if d == 0:
                    op("dve", lambda e, h_t=h_t: e.tensor_tensor_scan(
                        out=h_t[:, 0:NCTX], data0=rr[:, 0:NCTX], data1=ii[:, 0:NCTX], initial=0.0,
                        op0=ALU.mult, op1=ALU.add), reads=[b_rr, b_ii], writes=[b_h])
                    op("dve", lambda e, h_t=h_t: e.tensor_tensor_scan(
                        out=h_t[:, NCTX:NOUT], data0=rr[:, NCTX:NOUT], data1=ii[:, NCTX:NOUT],
                        initial=h_t[:, NCTX - 1:NCTX], op0=ALU.mult, op1=ALU.add),
                       reads=[b_rr, b_ii, b_h], writes=[b_h])
                else:
                    op("dve", lambda e, h_t=h_t: e.tensor_tensor_scan(
                        out=h_t[:, 0:NCTX][:, ::-1], data0=rr[:, 0:NCTX][:, ::-1], data1=ii[:, 0:NCTX][:, ::-1],
                        initial=0.0, op0=ALU.mult, op1=ALU.add), reads=[b_rr, b_ii], writes=[b_h])
                    op("dve", lambda e, h_t=h_t: e.tensor_tensor_scan(
                        out=h_t[:, NCTX:SEQ][:, ::-1], data0=rr[:, NCTX:SEQ][:, ::-1], data1=ii[:, NCTX:SEQ][:, ::-1],
                        initial=h_t[:, 0:1], op0=ALU.mult, op1=ALU.add),
                       reads=[b_rr, b_ii, b_h], writes=[b_h])
            bo, b_bo, sbo = bo_r.next()
            h0, b_h0 = hh[0]
            h1, b_h1 = hh[1]
            op("dve", lambda e, h0=h0, h1=h1: e.tensor_tensor(out=h0[:, 0:NOUT], in0=h0[:, 0:NOUT], in1=h1[:, 0:NOUT], op=ALU.add),
               reads=[b_h0, b_h1], writes=[b_h0])
            op("dve", lambda e, bo=bo, h0=h0, gy=gy: e.tensor_tensor(out=bo[:], in0=h0[:, 0:NOUT], in1=gy[:], op=ALU.mult),
               reads=[b_h0, b_gy], writes=[b_bo])
            dma("sp", lambda e, bo=bo, g=g: e.dma_start(out=mixT[1024 + g * 128:1024 + (g + 1) * 128, :], in_=bo[:]),
                sbo, reads=[b_bo], writes=[b_mixT])
        P.phase_end()

    if upto >= 3:
        l0c()
        mod_parts(0, [2, 3, 4, 5])
    if upto >= 4:
        proj_resid_phase("l0d", 0, "w_out", mixT, b_mixT, lambda s0: s0, xT_in, b_in, xA, b_xA, L0_TILES)
    if upto >= 5:
        mlp_phase(0, xA, b_xA, xB, b_xB, L0_TILES)
        mod_parts(1, [0, 1, 2, 3, 4, 5])

    def l1a():
        P.phase_begin()
        TM = 512
        hTs = [P.sb(f"hT{i}", [128, KC, TM], BF16) for i in range(2)]
        rstds = [P.sb(f"rstd{i}", [128, TM], F32) for i in range(2)]
        xch = Rot([P.sb(f"xch{i}", [128, TM], F32) + (P.sem(f"d_x{i}"),) for i in range(4)])
        sq = Rot(P.sbn("sq", [128, TM], BF16, 2))
        wts = WPool([P.sb(f"w{i}", [128, KC, 128], BF16) + (P.sem(f"d_w{i}"),) for i in range(8)])
        wtk = Rot([P.sb(f"wk{i}", [128, 4, KC, 128], BF16) + (P.sem(f"d_wk{i}"),) for i in range(2)])
        obh = Rot([P.sb(f"obh{i}", [128, TM], BF16) + (P.sem(f"d_obh{i}"),) for i in range(4)])
        vob = Rot([P.sb(f"vob{i}", [128, 512], BF16) + (P.sem(f"d_vob{i}"),) for i in range(3)])
        psr = Rot(list(zip(ps, b_ps)))
        def do_norm(s0, T, slot):
            hT, b_hT = hTs[slot]
            rstd, b_rstd = rstds[slot]
            norm_tile(xB, b_xB, s0, T, mod_ranges(1, 0, s0, T), hT, b_hT, xch, sq, rstd, b_rstd,
                      [psr.next(), psr.next()])

        def do_tile(s0, T, slot, nxt_tile):
            own = (s0, T) in L1_TILES
            hT, b_hT = hTs[slot]

            def cp_epi(dst, b_dst, base, d0, scale):
                def epi(g, pls):
                    n = g[0] - base
                    o_t, b_o, so = obh.next()
                    for (pst, b_pst, c0, pc) in pls[0]:
                        op("act", lambda e, pst=pst, c0=c0, pc=pc: e.activation(
                            out=o_t[:, c0:c0 + pc], in_=pst[:, 0:pc], func=AF.Copy, scale=scale),
                           reads=[b_pst], writes=[b_o])
                    dma("sp", lambda e: e.dma_start(out=dst[n * 128:(n + 1) * 128, d0:d0 + T], in_=o_t[:, 0:T]),
                        so, reads=[b_o], writes=[b_dst])
                return epi

            if own:
                lin("w_qkv", KC, [[n] for n in range(16)], hT, b_hT, T,
                    cp_epi(q1T, b_q1T, 0, s0 - NCTX, 128.0 ** -0.5), wts, psr,
                    nxt=[("w_qkv", 16 + n) for n in range(3)])
            lin("w_qkv", KC, [[16 + n] for n in range(16)], hT, b_hT, T, cp_epi(k1T, b_k1T, 16, s0, 1.0), wts, psr,
                nxt=nxt_tile)

            def v_epi(gi, tb, pst, b_pst):
                o_t, b_o, so = vob.next()
                op("act", lambda e: e.activation(out=o_t[:], in_=pst[:], func=AF.Copy), reads=[b_pst], writes=[b_o])
                dma("sp", lambda e: e.dma_start(
                    out=v1S[s0 + tb * 128:s0 + (tb + 1) * 128, gi * 512:(gi + 1) * 512], in_=o_t[:]),
                    so, reads=[b_o], writes=[b_v1S])

            lin_tok("w_qkv", 32, 4, hT, b_hT, T, v_epi, wtk, psr)

        tl = L1_TILES + L1_KV_TILES
        do_norm(tl[0][0], tl[0][1], 0)
        for ti, (s0_, T_) in enumerate(tl):
            if ti + 1 < len(tl):
                do_norm(tl[ti + 1][0], tl[ti + 1][1], (ti + 1) % 2)
            if ti + 1 == len(tl):
                nt = []
            elif tl[ti + 1] in L1_TILES:
                nt = [("w_qkv", n) for n in range(3)]
            else:
                nt = [("w_qkv", 16 + n) for n in range(3)]
            do_tile(s0_, T_, ti % 2, nt)
        P.phase_end()

    def l1b():
        P.phase_begin()
        lists = na_lists()
        NKB = NOUT // 128
        q_r = Rot([P.sb(f"nq{i}", [128, NOWN], BF16) + (P.sem(f"e_q{i}"),) for i in range(2)])
        k_r = Rot([P.sb(f"nk{i}", [128, NOUT], BF16) + (P.sem(f"e_k{i}"),) for i in range(2)])
        v_r = Rot([P.sb(f"nv{i}", [128, NKB, 128], BF16) + (P.sem(f"e_v{i}"),) for i in range(2)])
        bf_r = Rot([P.sb(f"nbf{i}", [128, 13, 128], F32) + (P.sem(f"e_b{i}"),) for i in range(2)])
        bb_r = Rot(P.sbn("nbb", [128, 13, 128], BF16, 2))
        o_r = Rot([P.sb(f"no{i}", [128, NOWN], BF16) + (P.sem(f"e_o{i}"),) for i in range(2)])
        pT_r = Rot(P.sbn("npT", [128, 7, 128], BF16, 3))
        rd_r = Rot(P.sbn("nrd", [128, 128], F32, 3))
        sbanks = [((ps[0], b_ps[0]), (ps[1], b_ps[1])), ((ps[2], b_ps[2]), (ps[3], b_ps[3])),
                  ((ps[4], b_ps[4]), (ps[5], b_ps[5]))]
        obanks = [(ps[6], Buf("po6")), (ps[7], Buf("po7"))]
        cnt = [0]
        prev = [None]

        def att_front(i, q_t, b_q, k_t, b_k, bb, b_bb):
            kl = lists[i]
            (pa, b_pa), (pb, b_pb) = sbanks[cnt[0] % 3]
            po, b_po = obanks[cnt[0] % 2]
            cnt[0] += 1
            pT, b_pT = pT_r.next()
            for idx, (kb, bid) in enumerate(kl):
                pst, b_pst = (pa, b_pa) if idx < 4 else (pb, b_pb)
                col = (idx % 4) * 128
                op("pe", lambda e, pst=pst, col=col, kb=kb, bid=bid: e.matmul(
                    out=pst[:, col:col + 128], lhsT=k_t[:, kb * 128:(kb + 1) * 128], rhs=q_t[:, i * 128:(i + 1) * 128],
                    start=True, stop=(bid is None)), reads=[b_k, b_q], writes=[b_pst])
                if bid is not None:
                    op("pe", lambda e, pst=pst, col=col, bid=bid: e.matmul(
                        out=pst[:, col:col + 128], lhsT=ident_b[:], rhs=bb[:, bid, :], start=False, stop=True),
                       reads=[b_identb, b_bb], writes=[b_pst])
            nk = len(kl)
            op("act", lambda e: e.activation(
                out=pT[:, 0:4, :], in_=pa[:, :].rearrange("p (g j) -> p g j", j=128), func=AF.Exp),
               reads=[b_pa], writes=[b_pT])
            op("act", lambda e: e.activation(
                out=pT[:, 4:nk, :], in_=pb[:, 0:(nk - 4) * 128].rearrange("p (g j) -> p g j", j=128), func=AF.Exp),
               reads=[b_pb], writes=[b_pT])
            return dict(kl=kl, nk=nk, pT=pT, b_pT=b_pT, po=po, b_po=b_po)

        def att_back(c):
            kl, nk, pT, b_pT, po, b_po = c["kl"], c["nk"], c["pT"], c["b_pT"], c["po"], c["b_po"]
            i, h, v_t, b_v, o_t, b_o, so = c["i"], c["h"], c["v_t"], c["b_v"], c["o_t"], c["b_o"], c["so"]
            for idx, (kb, bid) in enumerate(kl):
                op("pe", lambda e, kb=kb, idx=idx: e.matmul(
                    out=po[:, 0:128], lhsT=v_t[:, kb, :], rhs=pT[:, idx, :], start=(idx == 0), stop=(idx == nk - 1)),
                   reads=[b_v, b_pT], writes=[b_po])
            for idx, (kb, bid) in enumerate(kl):
                op("pe", lambda e, idx=idx: e.matmul(
                    out=po[:, 128:256], lhsT=ones1_bf[:], rhs=pT[:, idx, :], start=(idx == 0), stop=(idx == nk - 1)),
                   reads=[b_ones1, b_pT], writes=[b_po])
            rd, b_rd = rd_r.next()
            op("dve", lambda e: e.reciprocal(out=rd[:], in_=po[:, 128:256]), reads=[b_po], writes=[b_rd])
            op("dve", lambda e: e.tensor_tensor(
                out=o_t[:, i * 128:(i + 1) * 128], in0=po[:, 0:128], in1=rd[:], op=ALU.mult),
               reads=[b_po, b_rd], writes=[b_o])
            if i == 15:
                dma("sp", lambda e: e.dma_start(out=at1T[h * 128:(h + 1) * 128, :], in_=o_t[:]),
                    so, reads=[b_o], writes=[b_at1T])

        for h in range(16):
            q_t, b_q, sq_ = q_r.next()
            dma("sp", lambda e, q_t=q_t, h=h: e.dma_start(out=q_t[:], in_=q1T[h * 128:(h + 1) * 128, :]),
                sq_, reads=[b_q1T], writes=[b_q])
            k_t, b_k, sk = k_r.next()
            dma("sp", lambda e, k_t=k_t, h=h: e.dma_start(out=k_t[:], in_=k1T[h * 128:(h + 1) * 128, :]),
                sk, reads=[b_k1T], writes=[b_k])
            v_t, b_v, sv = v_r.next()
            dma("sp", lambda e, v_t=v_t, h=h: e.dma_start(
                out=v_t[:], in_=v1S[:, h * 128:(h + 1) * 128].rearrange("(b p) v -> p b v", p=128)),
                sv, reads=[b_v1S], writes=[b_v])
            bf, b_bf, sbf = bf_r.next()
            dma("sp", lambda e, bf=bf, h=h: e.dma_start(out=bf[:], in_=nab_in[h]), sbf, reads=[b_in], writes=[b_bf])
            bb, b_bb = bb_r.next()
            op("pool", lambda e, bb=bb, bf=bf: e.tensor_copy(out=bb[:], in_=bf[:]), reads=[b_bf], writes=[b_bb])
            o_t, b_o, so = o_r.next()
            for i in range(16):
                c = att_front(i, q_t, b_q, k_t, b_k, bb, b_bb)
                c.update(i=i, h=h, v_t=v_t, b_v=b_v, o_t=o_t, b_o=b_o, so=so)
                if prev[0] is not None:
                    att_back(prev[0])
                prev[0] = c
        att_back(prev[0])
        P.phase_end()

    def l1e():
        P.phase_begin()
        T = 512
        xch = Rot([P.sb(f"xch{i}", [128, T], F32) + (P.sem(f"f_x{i}"),) for i in range(4)])
        sq = Rot(P.sbn("sq", [128, T], BF16, 2))
        rstd, b_rstd = P.sb("rstd", [128, T], F32)
        xn_r = Rot(P.sbn("xn", [128, T], F32, 3))
        ot = [P.sb(f"ot{i}", [128, D], F32) + (P.sem(f"f_o{i}"),) for i in range(4)]
        psr = Rot(list(zip(ps, b_ps)))
        ident_f = umat[:, 4, :]
        def do_tile(ti):
            s0 = NCTX + ti * T
            sumsq_rstd(xB, b_xB, s0, T, xch, sq, rstd, b_rstd, [psr.next()], ones_bf, b_ones, KC)
            for cg in range(4):
                pbanks = [psr.next() for _ in range(4)]
                for cc in range(4):
                    c = cg * 4 + cc
                    x_t, b_x, sx = xch.next()
                    dma("sp", lambda e, x_t=x_t, c=c: e.dma_start(out=x_t[:], in_=xB[c * 128:(c + 1) * 128, s0:s0 + T]),
                        sx, reads=[b_xB], writes=[b_x])
                    xn, b_xn = xn_r.next()
                    op("dve", lambda e, xn=xn, x_t=x_t, c=c: e.scalar_tensor_tensor(
                        out=xn[:], in0=x_t[:], scalar=ngt[:, 4, c:c + 1], in1=rstd[:], op0=ALU.mult, op1=ALU.mult),
                       reads=[b_x, b_ngt, b_rstd], writes=[b_xn])
                    for tb in range(4):
                        pst, b_pst = pbanks[tb]
                        op("pe", lambda e, pst=pst, xn=xn, tb=tb, cc=cc: e.matmul(
                            out=pst[:, cc * 128:(cc + 1) * 128], lhsT=xn[:, tb * 128:(tb + 1) * 128], rhs=ident_f,
                            start=True, stop=True), reads=[b_xn, b_umat], writes=[b_pst])
                for tb in range(4):
                    pst, b_pst = pbanks[tb]
                    o_t, b_o, so = ot[tb]
                    eng = "act" if tb % 2 == 0 else "dve"
                    if eng == "act":
                        op("act", lambda e, o_t=o_t, pst=pst, cg=cg: e.activation(
                            out=o_t[:, cg * 512:(cg + 1) * 512], in_=pst[:], func=AF.Copy), reads=[b_pst], writes=[b_o])
                    else:
                        op("dve", lambda e, o_t=o_t, pst=pst, cg=cg: e.tensor_copy(
                            out=o_t[:, cg * 512:(cg + 1) * 512], in_=pst[:]), reads=[b_pst], writes=[b_o])
            for tb in range(4):
                o_t, b_o, so = ot[tb]
                r0 = ti * T + tb * 128
                dma("sp", lambda e, o_t=o_t, r0=r0: e.dma_start(out=out_d[r0:r0 + 128, :], in_=o_t[:]),
                    so, reads=[b_o], writes=[b_out])

        for ti_ in range(NOWN // T):
            do_tile(ti_)
        P.phase_end()

    if upto >= 6:
        l1a()
    if upto >= 7:
        l1b()
    if upto >= 8:
        proj_resid_phase("l1c", 1, "w_no", at1T, b_at1T, lambda s0: s0 - NCTX, xB, b_xB, xA, b_xA, L1_TILES)
    if upto >= 9:
        mlp_phase(1, xA, b_xA, xB, b_xB, L1_TILES)
    if upto >= 10:
        l1e()
    P.phase_begin()
    P.phase_end()
    P.top.close()
    return P


def _cm(W):
    K, N = W.shape
    return np.ascontiguousarray(W.reshape(K // 128, 128, N // 128, 128).transpose(2, 1, 0, 3))


def _col(v):
    return np.ascontiguousarray(v.reshape(-1, 128).T)


def _rope_tables(flip):
    n_freq = 32
    inv_freq = (np.float32(10000.0) ** (-np.arange(n_freq, dtype=np.float32) / np.float32(n_freq))).astype(np.float32)
    t = np.arange(NLAT)
    if flip:
        t = NLAT - 1 - t
    row = (t // GRID_W).astype(np.float32)
    col = (t % GRID_W).astype(np.float32)
    p = np.arange(128)
    pos = np.where((p // 64)[:, None] == 0, row[None, :], col[None, :]).astype(np.float32)
    ang = (pos * inv_freq[p % 32][:, None]).astype(np.float32)
    sign = np.where((p % 64) < 32, -1.0, 1.0).astype(np.float32)[:, None]
    cos = np.cos(ang).astype(np.float32)
    sin = (np.sin(ang).astype(np.float32) * sign).astype(np.float32)
    tab = np.zeros((4, 128, SEQ), np.float32)
    sc = np.float32(128.0 ** -0.5)
    tab[0, :, :NCTX] = sc
    tab[2, :, :NCTX] = 1.0
    tab[0, :, NCTX:] = cos * sc
    tab[1, :, NCTX:] = sin * sc
    tab[2, :, NCTX:] = cos
    tab[3, :, NCTX:] = sin
    return tab


def _umat():
    s = np.arange(128)[:, None]
    t = np.arange(128)[None, :]
    same = (s // 64) == (t // 64)
    u = np.zeros((128, 6, 128), np.float32)
    u[:, 0, :] = np.where(same & (s <= t), -1.0 / 16.0, 0.0)
    u[:, 1, :] = np.where(same & (s >= t), -1.0 / 16.0, 0.0)
    u[:, 2, :] = np.where(same & (s <= t), 1.0, 0.0)
    u[:, 3, :] = np.where(same & (s >= t), 1.0, 0.0)
    u[:, 4, :] = np.eye(128, dtype=np.float32)
    u[:, 5, :] = 1.0
    return u


def _na_bias(rel_bias, flip):
    lists = na_lists()
    out = np.full((16, 128, 13, 128), NEG, np.float32)
    done = {}
    for i in range(16):
        for (kb, bid) in lists[i]:
            if bid is None:
                continue
            j = kb - 2
            ql = i * 128 + np.arange(128)
            kl = j * 128 + np.arange(128)
            qg = (NLAT - 1 - ql) if flip else ql
            kg = (NLAT - 1 - kl) if flip else kl
            qr, qc = qg // GRID_W, qg % GRID_W
            kr, kc = kg // GRID_W, kg % GRID_W
            rs = np.clip(qr - 4, 0, 64 - 8)
            cs = np.clip(qc - 8, 0, GRID_W - 16)
            valid = ((kr[:, None] >= rs[None, :]) & (kr[:, None] < rs[None, :] + 8) &
                     (kc[:, None] >= cs[None, :]) & (kc[:, None] < cs[None, :] + 16))
            dr = np.clip(kr[:, None] - qr[None, :] + 7, 0, 14)
            dc = np.clip(kc[:, None] - qc[None, :] + 15, 0, 30)
            blk = np.where(valid[None], rel_bias[:, dr, dc], np.float32(NEG)).astype(np.float32)
            if bid in done:
                assert np.array_equal(done[bid], blk), (i, j, bid)
            else:
                done[bid] = blk
                out[:, :, bid, :] = blk
    return out


_PROG = {}


def _get_prog():
    if "p" not in _PROG:
        _PROG["p"] = build()
    return _PROG["p"]


def make_in_maps(x, c, ctx, c_ctx, mod_w, mod_b, norm1_g, norm2_g, mlp_w_up, mlp_w_down,
                 ev_w_in, ev_w_out, gla_gate_up, gla_gate_b, gla_norm_g, lru_conv_w, lru_conv_b,
                 lru_w_a, lru_b_a, lru_w_x, lru_b_x, lru_lambda, na_w_qkv, na_w_out, na_rel_bias,
                 final_norm_g):
    f = lambda a: np.asarray(a, dtype=np.float32)
    x, c, ctx, c_ctx = f(x), f(c), f(ctx), f(c_ctx)
    shared = {}
    for l in range(2):
        shared[f"modw{l}"] = _cm(f(mod_w[l]))
        mb = _col(f(mod_b[l]))
        shared[f"modb{l}"] = np.ascontiguousarray(np.stack([mb, mb], axis=-1))
        shared[f"w_up{l}"] = _cm(f(mlp_w_up[l]))
        shared[f"w_dn{l}"] = _cm(f(mlp_w_down[l]))
    shared["normg"] = np.ascontiguousarray(np.stack(
        [_col(f(norm1_g[0])), _col(f(norm2_g[0])), _col(f(norm1_g[1])), _col(f(norm2_g[1])), _col(f(final_norm_g))], axis=1))
    shared["w_out"] = _cm(f(ev_w_out[0]))
    shared["w_qkv"] = _cm(f(na_w_qkv[0]))
    shared["w_no"] = _cm(f(na_w_out[0]))
    shared["umat"] = _umat()
    shared["gla_g"] = _col(f(gla_norm_g[0]))
    W = f(ev_w_in[0])
    p = np.arange(128)
    partner = np.where((p % 64) < 32, p + 32, p - 32)
    perm512 = (np.arange(4)[:, None] * 128 + partner[None, :]).reshape(-1)
    wq, wk = W[:, 0:512], W[:, 512:1024]
    wv, wog = W[:, 1024:2048], W[:, 2048:3072]
    wgl, wxr, wyr = W[:, 3072:3104], W[:, 3104:4128], W[:, 4128:5152]
    per_half = []
    for half in range(2):
        flip = half == 1
        dirs = [1, 0] if flip else [0, 1]
        gl = np.zeros((D, 128), np.float32)
        gl[:, 0:16] = wgl[:, dirs[0] * 16:dirs[0] * 16 + 16]
        gl[:, 16:32] = wgl[:, dirs[1] * 16:dirs[1] * 16 + 16]
        wext = np.concatenate([wq, wq[:, perm512], wk, wk[:, perm512], wv, wog, wxr, wyr, gl], axis=1)
        hd = {"w_in": _cm(wext), "rope": _rope_tables(flip)}
        gu = np.zeros((17, 2, 512), np.float32)
        for di, dr in enumerate(dirs):
            gu[0:16, di, :] = f(gla_gate_up[0, dr])
            gu[16, di, :] = f(gla_gate_b[0, dr])
        hd["gate_up"] = gu
        lw = np.zeros((128, 32, 128), np.float32)
        lb = np.zeros((128, 32), np.float32)
        lam = np.zeros((128, 16), np.float32)
        for di, dr in enumerate(dirs):
            for g in range(8):
                lw[:, di * 16 + g, :] = f(lru_w_a[0, dr, g])
                lw[:, di * 16 + 8 + g, :] = f(lru_w_x[0, dr, g])
                lb[:, di * 16 + g] = f(lru_b_a[0, dr, g * 128:(g + 1) * 128])
                lb[:, di * 16 + 8 + g] = f(lru_b_x[0, dr, g * 128:(g + 1) * 128])
                lam[:, di * 8 + g] = f(lru_lambda[0, dr, g * 128:(g + 1) * 128])
        hd["lru_w"], hd["lru_b"], hd["lru_lam"] = lw, lb, lam
        cw = np.zeros((128, 8, 6), np.float32)
        cwt = f(lru_conv_w[0])
        for g in range(8):
            sl = slice(g * 128, (g + 1) * 128)
            if not flip:
                for j in range(4):
                    cw[:, g, j] = cwt[j, sl]
            else:
                for j in range(4):
                    cw[:, g, 4 - j] = cwt[j, sl]
            cw[:, g, 5] = f(lru_conv_b[0])[sl]
        hd["conv"] = cw
        hd["na_bias"] = _na_bias(f(na_rel_bias[0]), flip)
        per_half.append(hd)
    in_maps = []
    for core in range(8):
        b, half = core // 2, core % 2
        xl = x[b][::-1] if half else x[b]
        cl = ctx[b][::-1] if half else ctx[b]
        m = dict(shared)
        m.update(per_half[half])
        m["xT"] = np.ascontiguousarray(np.concatenate([cl, xl], axis=0).T)
        m["cT"] = np.ascontiguousarray(np.stack([_col(c[b]), _col(c_ctx)], axis=-1))
        in_maps.append(m)
    return in_maps


def kernel(**inputs):
    P = _get_prog()
    in_maps = make_in_maps(**inputs)
    res = run_bass_kernel_spmd(P.nc, in_maps, core_ids=list(range(8)))
    out = np.zeros((4, NLAT, D), np.float32)
    for core in range(8):
        b, half = core // 2, core % 2
        o = np.asarray(res.results[core]["out"])
        if half:
            out[b, NLAT - NOWN:] = o[::-1]
        else:
            out[b, :NOWN] = o
    return out
```
